# Optimizing a Trainium2 kernel written in Bass

```python
import math
import jax, jax.numpy as jnp
from jax import lax
import numpy as np


D_MODEL = 1024
BATCH = 16
SEQ = 4096
DEPTH = 2
DEC_BATCH = 2
DEC_SEQ = 16384
PAST_LEN = 128

GRID_W = 64
NA_HEADS = 16
NA_HEAD_DIM = D_MODEL // NA_HEADS
NA_KH_MAX = 8
NA_KW = 16
NA_QCB = NA_KW
NA_KCB = 2 * NA_KW
DIFF_HEAD_DIM = 64
DIFF_HEADS = D_MODEL // (2 * DIFF_HEAD_DIM)
Q_BLOCK = 128
T5_BUCKETS = 32
T5_MAX_DIST = 128
D_FF = -(-(8 * D_MODEL) // (3 * 256)) * 256
N_NA_LAYERS = (DEPTH + 1) // 2
N_DIFF_LAYERS = DEPTH // 2
RMS_EPS = 1e-6
NEG_INF = -1e30

kernel_name = "hybrid_natten_diffattn_encoder"


def rms_norm(x, g):
    xf = x.astype(jnp.float32)
    y = xf * lax.rsqrt(jnp.mean(xf * xf, axis=-1, keepdims=True) + RMS_EPS)
    return (y * g.astype(jnp.float32)).astype(x.dtype)


def swiglu(x, w_gate, w_up, w_down):
    return (jax.nn.silu(x @ w_gate) * (x @ w_up)) @ w_down


def neighborhood_attention(x, w_qkv, w_o, rpb):
    b, s, _ = x.shape
    rows = s // GRID_W
    kh = min(NA_KH_MAX, rows)
    ncb = GRID_W // NA_QCB
    qkv = (x @ w_qkv).reshape(b, rows, GRID_W, 3, NA_HEADS, NA_HEAD_DIM)
    q = qkv[:, :, :, 0] * (NA_HEAD_DIM ** -0.5)
    k = qkv[:, :, :, 1]
    v = qkv[:, :, :, 2]
    r = jnp.arange(rows)
    row0 = jnp.clip(r - kh // 2, 0, rows - kh)
    key_rows = row0[:, None] + jnp.arange(kh)[None, :]
    dr_idx = key_rows - r[:, None] + (NA_KH_MAX - 1)
    j = jnp.arange(ncb)
    q_cols = j[:, None] * NA_QCB + jnp.arange(NA_QCB)[None, :]
    col0 = jnp.clip(j * NA_QCB - NA_KW // 2, 0, GRID_W - NA_KCB)
    k_cols = col0[:, None] + jnp.arange(NA_KCB)[None, :]
    q_start = jnp.clip(q_cols - NA_KW // 2, 0, GRID_W - NA_KW)
    kc = k_cols[:, None, :]
    in_win = (kc >= q_start[:, :, None]) & (kc < q_start[:, :, None] + NA_KW)
    dc_idx = jnp.clip(kc - q_cols[:, :, None] + (NA_KW - 1), 0, 2 * NA_KW - 2)
    rpb_f = rpb.astype(jnp.float32)

    def row_step(args):
        q_r, rows_r, dr_r = args
        k_blk = k[:, rows_r[:, None, None], k_cols[None, :, :]]
        v_blk = v[:, rows_r[:, None, None], k_cols[None, :, :]]
        bias = rpb_f[:, dr_r][:, :, dc_idx]
        bias = jnp.transpose(bias, (0, 2, 3, 1, 4))
        bias = jnp.where(in_win[:, :, None, :], bias, NEG_INF)
        sc = jnp.einsum('bjqhd,bkjchd->bhjqkc', q_r, k_blk).astype(jnp.float32) + bias
        p = jax.nn.softmax(sc.reshape(b, NA_HEADS, ncb, NA_QCB, kh * NA_KCB), axis=-1)
        p = p.reshape(sc.shape).astype(v.dtype)
        return jnp.einsum('bhjqkc,bkjchd->bjqhd', p, v_blk)

    q_rows = jnp.moveaxis(q.reshape(b, rows, ncb, NA_QCB, NA_HEADS, NA_HEAD_DIM), 1, 0)
    out = lax.map(row_step, (q_rows, key_rows, dr_idx))
    out = jnp.moveaxis(out, 0, 1).reshape(b, s, D_MODEL)
    return out @ w_o


def t5_bucket(rel):
    nb = T5_BUCKETS // 2
    max_exact = nb // 2
    ret = jnp.where(rel > 0, nb, 0)
    n = jnp.abs(rel)
    nf = jnp.maximum(n, max_exact).astype(jnp.float32)
    large = max_exact + (jnp.log(nf / max_exact) / math.log(T5_MAX_DIST / max_exact)
                         * (nb - max_exact)).astype(jnp.int32)
    large = jnp.minimum(large, nb - 1)
    return ret + jnp.where(n < max_exact, n, large)


def diff_attention(x, w_qkv, w_o, lq1, lk1, lq2, lk2, subln_g, t5_bias, lambda_init):
    b, s, _ = x.shape
    nblk = s // Q_BLOCK
    q, k, v = jnp.split(x @ w_qkv, 3, axis=-1)
    q = (q * DIFF_HEAD_DIM ** -0.5).reshape(b, nblk, Q_BLOCK, DIFF_HEADS, 2, DIFF_HEAD_DIM)
    qb = jnp.transpose(q, (1, 4, 0, 3, 2, 5))
    k = k.reshape(b, s, DIFF_HEADS, 2, DIFF_HEAD_DIM)
    k1 = jnp.transpose(k[:, :, :, 0], (0, 2, 1, 3))
    k2 = jnp.transpose(k[:, :, :, 1], (0, 2, 1, 3))
    v = jnp.transpose(v.reshape(b, s, DIFF_HEADS, 2 * DIFF_HEAD_DIM), (0, 2, 1, 3))
    f32 = jnp.float32
    lam = (jnp.exp(jnp.sum(lq1.astype(f32) * lk1.astype(f32)))
           - jnp.exp(jnp.sum(lq2.astype(f32) * lk2.astype(f32))) + lambda_init)
    k_pos = jnp.arange(s)
    table = t5_bias.astype(f32)

    def blk_step(args):
        q_blk, i = args
        q_pos = i * Q_BLOCK + jnp.arange(Q_BLOCK)
        bias = jnp.transpose(table[t5_bucket(k_pos[None, :] - q_pos[:, None])], (2, 0, 1))
        a1 = jax.nn.softmax(jnp.einsum('bhqd,bhkd->bhqk', q_blk[0], k1).astype(f32) + bias, axis=-1)
        a2 = jax.nn.softmax(jnp.einsum('bhqd,bhkd->bhqk', q_blk[1], k2).astype(f32) + bias, axis=-1)
        w = (a1 - lam * a2).astype(v.dtype)
        return jnp.einsum('bhqk,bhke->bhqe', w, v)

    out = lax.map(blk_step, (qb, jnp.arange(nblk)))
    out = rms_norm(out, subln_g) * (1.0 - lambda_init)
    out = jnp.transpose(out, (1, 0, 3, 2, 4)).reshape(b, s, D_MODEL)
    return out @ w_o


def encoder_trunk(x, mix_pre_g, mix_post_g, na_w_qkv, na_w_o, na_rpb, diff_w_qkv, diff_w_o,
                  diff_lambda_q1, diff_lambda_k1, diff_lambda_q2, diff_lambda_k2, diff_subln_g,
                  t5_bias, ffn_pre_g, ffn_post_g, ffn_w_gate, ffn_w_up, ffn_w_down):
    for i in range(DEPTH):
        h = rms_norm(x, mix_pre_g[i])
        li = i // 2
        if i % 2 == 0:
            m = neighborhood_attention(h, na_w_qkv[li], na_w_o[li], na_rpb[li])
        else:
            lambda_init = 0.8 - 0.6 * math.exp(-0.3 * i)
            m = diff_attention(h, diff_w_qkv[li], diff_w_o[li], diff_lambda_q1[li], diff_lambda_k1[li],
                               diff_lambda_q2[li], diff_lambda_k2[li], diff_subln_g[li], t5_bias, lambda_init)
        x = x + rms_norm(m, mix_post_g[i])
        h = rms_norm(x, ffn_pre_g[i])
        x = x + rms_norm(swiglu(h, ffn_w_gate[i], ffn_w_up[i], ffn_w_down[i]), ffn_post_g[i])
    return x


def setup_inputs(seed: int = 0) -> dict:
    key = jax.random.key(seed)
    ks = jax.random.split(key, 20)
    n = jax.random.normal
    f32 = jnp.float32
    D = D_MODEL
    return {
        "x_prompt": n(ks[0], (BATCH, SEQ, D), f32),
        "x_sample": n(ks[1], (DEC_BATCH, DEC_SEQ, D), f32),
        "mix_pre_g": 1.0 + 0.05 * n(ks[2], (DEPTH, D), f32),
        "mix_post_g": 1.0 + 0.05 * n(ks[3], (DEPTH, D), f32),
        "na_w_qkv": n(ks[4], (N_NA_LAYERS, D, 3 * D), f32) * D ** -0.5,
        "na_w_o": n(ks[5], (N_NA_LAYERS, D, D), f32) * D ** -0.5,
        "na_rpb": 0.2 * n(ks[6], (N_NA_LAYERS, NA_HEADS, 2 * NA_KH_MAX - 1, 2 * NA_KW - 1), f32),
        "diff_w_qkv": n(ks[7], (N_DIFF_LAYERS, D, 3 * D), f32) * D ** -0.5,
        "diff_w_o": n(ks[8], (N_DIFF_LAYERS, D, D), f32) * D ** -0.5,
        "diff_lambda_q1": 0.1 * n(ks[9], (N_DIFF_LAYERS, DIFF_HEAD_DIM), f32),
        "diff_lambda_k1": 0.1 * n(ks[10], (N_DIFF_LAYERS, DIFF_HEAD_DIM), f32),
        "diff_lambda_q2": 0.1 * n(ks[11], (N_DIFF_LAYERS, DIFF_HEAD_DIM), f32),
        "diff_lambda_k2": 0.1 * n(ks[12], (N_DIFF_LAYERS, DIFF_HEAD_DIM), f32),
        "diff_subln_g": 1.0 + 0.05 * n(ks[13], (N_DIFF_LAYERS, 2 * DIFF_HEAD_DIM), f32),
        "t5_bias": 0.2 * n(ks[14], (T5_BUCKETS, DIFF_HEADS), f32),
        "ffn_pre_g": 1.0 + 0.05 * n(ks[15], (DEPTH, D), f32),
        "ffn_post_g": 1.0 + 0.05 * n(ks[16], (DEPTH, D), f32),
        "ffn_w_gate": n(ks[17], (DEPTH, D, D_FF), f32) * D ** -0.5,
        "ffn_w_up": n(ks[18], (DEPTH, D, D_FF), f32) * D ** -0.5,
        "ffn_w_down": n(ks[19], (DEPTH, D_FF, D), f32) * D_FF ** -0.5,
    }


def reference(x_prompt, x_sample, mix_pre_g, mix_post_g, na_w_qkv, na_w_o, na_rpb, diff_w_qkv, diff_w_o,
              diff_lambda_q1, diff_lambda_k1, diff_lambda_q2, diff_lambda_k2, diff_subln_g, t5_bias,
              ffn_pre_g, ffn_post_g, ffn_w_gate, ffn_w_up, ffn_w_down):
    y_prompt = encoder_trunk(x_prompt, mix_pre_g, mix_post_g, na_w_qkv, na_w_o, na_rpb, diff_w_qkv, diff_w_o,
                             diff_lambda_q1, diff_lambda_k1, diff_lambda_q2, diff_lambda_k2, diff_subln_g,
                             t5_bias, ffn_pre_g, ffn_post_g, ffn_w_gate, ffn_w_up, ffn_w_down)
    y_sample = encoder_trunk(x_sample, mix_pre_g, mix_post_g, na_w_qkv, na_w_o, na_rpb, diff_w_qkv, diff_w_o,
                             diff_lambda_q1, diff_lambda_k1, diff_lambda_q2, diff_lambda_k2, diff_subln_g,
                             t5_bias, ffn_pre_g, ffn_post_g, ffn_w_gate, ffn_w_up, ffn_w_down)
    return (y_prompt, y_sample)
```

```python
import math
import numpy as np
import ml_dtypes
import concourse.bass as bass
import concourse.mybir as mybir
from concourse.bass_utils import run_bass_kernel_spmd

F32 = mybir.dt.float32
BF16 = mybir.dt.bfloat16
ALU = mybir.AluOpType
AF = mybir.ActivationFunctionType
BF = ml_dtypes.bfloat16

D = 1024
DFF = 2816
NFC = DFF // 128
NEG = -1e30
NCORES = 8


class _Op:
    __slots__ = ("eng", "fn", "tl", "pos", "waits", "flag", "vc", "dma", "count", "inc")


class Sched:
    ENGS = ("pe", "act", "dve", "pool", "sp")

    def __init__(self, nc, ndma=None):
        self.nc = nc
        ndma = ndma or {"sp": 16, "pool": 8, "act": 4}
        self.ops = {e: [] for e in self.ENGS}
        self.tl_ops = {e: [] for e in ("pe", "act", "dve", "pool")}
        self.dma_tls = {}
        for q, n in ndma.items():
            self.dma_tls[q] = [f"d_{q}{i}" for i in range(n)]
            for t in self.dma_tls[q]:
                self.tl_ops[t] = []
        self.dma_rr = {q: 0 for q in ndma}
        self.clock = {e: {} for e in self.ENGS}
        self.lastw = {}
        self.readers = {}

    def _need(self, eng, X, Y, raw):
        if Y is None or Y is X:
            return
        if (not Y.dma) and Y.eng == eng and not X.dma:
            if eng == "pe" or not raw:
                return
        ck = self.clock[eng]
        if ck.get(Y.tl, 0) >= Y.pos:
            return
        Y.flag = True
        ck = dict(ck)
        for t, p in Y.vc.items():
            if ck.get(t, 0) < p:
                ck[t] = p
        if ck.get(Y.tl, 0) < Y.pos:
            ck[Y.tl] = Y.pos
        self.clock[eng] = ck
        if X.waits.get(Y.tl, 0) < Y.pos:
            X.waits[Y.tl] = Y.pos

    def add(self, eng, fn, reads=(), writes=(), dma=False, inc=16):
        X = _Op()
        X.eng, X.fn, X.dma, X.flag, X.waits = eng, fn, dma, False, {}
        X.inc = inc
        if dma:
            tls = self.dma_tls[eng]
            X.tl = tls[self.dma_rr[eng] % len(tls)]
            self.dma_rr[eng] += 1
        else:
            X.tl = eng
        lst = self.tl_ops[X.tl]
        X.pos = len(lst) + 1
        if dma and lst:
            self._need(eng, X, lst[-1], True)
        for b in reads:
            self._need(eng, X, self.lastw.get(b), True)
        for b in writes:
            self._need(eng, X, self.lastw.get(b), False)
            rd = self.readers.get(b)
            if rd:
                for Y in rd.values():
                    self._need(eng, X, Y, False)
        X.vc = self.clock[eng]
        lst.append(X)
        self.ops[eng].append(X)
        for b in writes:
            self.lastw[b] = X
            self.readers[b] = {}
        for b in reads:
            self.readers.setdefault(b, {})[X.tl] = X
        return X

    def barrier(self, dma_only_for=("sp",)):
        last = [lst[-1] for lst in self.tl_ops.values() if lst]
        for eng in self.ENGS:
            X = _Op()
            X.eng, X.fn, X.dma, X.flag, X.waits = eng, None, False, False, {}
            X.tl, X.pos = None, 0
            for Y in last:
                if (not Y.dma) and Y.eng == eng:
                    continue
                self._need(eng, X, Y, True)
            X.vc = self.clock[eng]
            self.ops[eng].append(X)
        self.lastw = {}
        self.readers = {}

    def emit(self):
        nc = self.nc
        sems = {}
        for t, lst in self.tl_ops.items():
            c = 0
            for op in lst:
                if op.dma:
                    c += op.inc
                elif op.flag:
                    c += 1
                op.count = c
            if lst:
                sems[t] = nc.alloc_semaphore(f"s_{t}")
        tl_ops = self.tl_ops

        def run(e, name):
            for op in self.ops[name]:
                for t, p in op.waits.items():
                    e.wait_ge(sems[t], tl_ops[t][p - 1].count)
                if op.fn is None:
                    continue
                inst = op.fn(e)
                if op.dma:
                    inst.then_inc(sems[op.tl], op.inc)
                elif op.flag:
                    inst.then_inc(sems[op.tl], 1)

        with nc.Block() as block:
            @block.tensor
            def _(e):
                run(e, "pe")

            @block.scalar
            def _(e):
                run(e, "act")

            @block.vector
            def _(e):
                run(e, "dve")

            @block.gpsimd
            def _(e):
                run(e, "pool")

            @block.sync
            def _(e):
                run(e, "sp")


def I(name, *args, **kw):
    return lambda e: getattr(e, name)(*args, **kw)

class Arena:
    LO, HI = 16512, 229300

    def __init__(self, nc):
        self.nc, self.off, self.n, self.base = nc, self.LO, 0, self.LO

    def t(self, shape, dtype):
        sz = 4 if dtype == F32 else 2
        nb = sz
        for s in shape[1:]:
            nb *= s
        nb = (nb + 63) // 64 * 64
        assert self.off + nb <= self.HI, f"SBUF overflow {self.off + nb}"
        self.n += 1
        h = self.nc.alloc_sbuf_tensor_at(f"sb{self.n}", list(shape), dtype, offset=self.off)
        self.off += nb
        return h

    def persist(self):
        self.base = self.off

    def reset(self):
        self.off = self.base


def _t5_bucket(rel):
    nb, me = 16, 8
    ret = np.where(rel > 0, nb, 0)
    n = np.abs(rel)
    nf = np.maximum(n, me).astype(np.float32)
    large = me + (np.log(nf / me) / np.float32(math.log(128 / me)) * (nb - me)).astype(np.int32)
    large = np.minimum(large, nb - 1)
    return ret + np.where(n < me, n, large)


T5C0, T5C = 768, 1536
NT5 = 25


def _static_tables(rows, core):
    T = rows * 64
    nqt = T // 512
    sl = core
    out = {}
    kc = np.arange(64)[:, None]
    c = np.arange(64)[None, :]
    dc = np.clip(kc - c + 15, 0, 30)
    oh = np.zeros((31, 64, 64), np.float32)
    for i in range(31):
        oh[i] = (dc == i)
    out["c_ohc"] = oh.reshape(31, 4096)
    qs = np.clip(c - 8, 0, 48)
    out["c_colmask"] = np.where((kc >= qs) & (kc < qs + 16), 0.0, NEG).astype(np.float32).reshape(1, 4096)
    rm = np.zeros((3, nqt, 2, 8, 8, 64), np.float32)
    for seg in range(3):
        rows_seq = rows if seg < 2 else 4 * rows
        for qt in range(nqt):
            if seg < 2:
                q0 = 8 * qt
            else:
                q0 = sl * (rows // 2) + 8 * (qt % (nqt // 2))
            for t in range(8):
                for a in range(2):
                    kr = q0 + 2 * t + a - 4
                    for b in range(8):
                        r = q0 + b
                        row0 = min(max(r - 4, 0), rows_seq - 8)
                        ok = (0 <= kr < rows_seq) and (row0 <= kr < row0 + 8)
                        rm[seg, qt, a, t, b, :] = 0.0 if ok else NEG
    out["c_rm"] = rm.reshape(3, nqt, 2, 8 * 512).astype(BF)
    ohr = np.zeros((2, 128), np.float32)
    ohr[0, :64] = 1
    ohr[1, 64:] = 1
    out["c_rmoh"] = ohr.astype(BF)
    cidx = np.arange(T5C)
    bt = _t5_bucket(T5C0 - cidx)
    true = np.zeros((32, T5C), np.float32)
    true[bt, cidx] = 1
    L = np.zeros((32, T5C), np.float32)
    L[15] = 1
    R = np.zeros((32, T5C), np.float32)
    R[31] = 1
    tabs = np.zeros((32, NT5, T5C), np.float32)
    tabs[:, 0] = true
    for r in range(8):
        tabs[:, 1 + r] = true if r == sl else (L if r < sl else R)
        tabs[:, 9 + r] = true if r == sl - 1 else (L if r < sl - 1 else R)
        tabs[:, 17 + r] = true if r == sl + 1 else (L if r <= sl else R)
    out["c_t5oh"] = tabs.reshape(32, NT5 * T5C)
    out["c_ident"] = np.eye(128, dtype=np.float32).astype(BF)
    return out


def _gcol(g):
    return np.ascontiguousarray(g.reshape(-1, 128).T)


def build_program(rows=64, debug_outs=(), use_cc=True, stop_after=None):
    T = rows * 64
    TE = T + 512
    TE2 = T + 1024
    TES = [TE, TE, TE2]
    CH = T // 2
    NQT = T // 512
    NQH = NQT // 2
    NKT = TE2 // 128
    nc = bass.Bass("TRN2", target_bir_lowering=False)
    S = Sched(nc)
    A = Arena(nc)

    def din(name, shape, dt=F32):
        return nc.dram_tensor(name, list(shape), dt, kind="ExternalInput")

    def dscr(name, shape, dt):
        kind = "ExternalOutput" if name in debug_outs else "Internal"
        return nc.dram_tensor(name, list(shape), dt, kind=kind)

    xin = din("xin", [3, TE2, D])
    w_qkv = [din("na_w_qkv", [D, 3 * D]), din("diff_w_qkv", [D, 3 * D])]
    w_o = [din("na_w_o", [D, D]), din("diff_w_o", [D, D])]
    w_gate = din("ffn_w_gate", [2, D, DFF])
    w_up = din("ffn_w_up", [2, D, DFF])
    w_down = din("ffn_w_down", [2, DFF, D])
    g_mix_pre = din("g_mix_pre", [2, 128, 8])
    g_ffn_pre = din("g_ffn_pre", [2, 128, 8])
    g_mix_post = din("g_mix_post", [2, D])
    g_ffn_post = din("g_ffn_post", [2, D])
    rpbT = din("rpbT", [31, 368])
    lamv = din("lamv", [1, 256])
    subg = din("subg", [128, 1])
    t5t = din("t5t", [32, 8])
    c_ohc = din("c_ohc", [31, 4096])
    c_colmask = din("c_colmask", [1, 4096])
    c_rm = din("c_rm", [3, NQT, 2, 8 * 512], BF16)
    c_rmoh = din("c_rmoh", [2, 128], BF16)
    c_t5oh = din("c_t5oh", [32, NT5 * T5C])
    c_ident = din("c_ident", [128, 128], BF16)
    yout = nc.dram_tensor("yout", [3, T, D], F32, kind="ExternalOutput")

    tfull = dscr("tfull", [368, 4096], F32)
    zt5 = dscr("zt5", [NT5 * 8, 130, T5C], F32)
    qT0 = dscr("qT0", [3, 8, 128, TE2], BF16)
    kT0 = dscr("kT0", [3, 8, 128, TE2], BF16)
    v0 = dscr("v0", [3, TE2, D], BF16)
    aT = [dscr("aT0", [3, 8, 128, T], BF16), dscr("aT1", [3, 8, 128, T], BF16)]
    x1 = dscr("x1", [3, T, D], F32)
    x2 = dscr("x2", [3, T, D], F32)
    x3 = dscr("x3", [3, T, D], F32)
    actT = dscr("actT", [3, NFC, 128, T], BF16)
    qT1 = dscr("qT1", [3, 8, 128, T], BF16)
    kT1 = dscr("kT1", [2, 8, 128, T], BF16)
    v1 = dscr("v1", [2, T, D], BF16)
    kT1s = dscr("kT1s", [8 * 128, T], BF16)
    v1s = dscr("v1s", [T, D], BF16)
    kTg = dscr("kTg", [8 * 8 * 128, T], BF16)
    vg = dscr("vg", [8 * T, D], BF16)

    ps = [nc.alloc_psum_tensor(f"ps{i}", [128, 512], F32) for i in range(8)]
    psb = [p.bitcast(BF16) for p in ps]

    idt = A.t([128, 128], BF16)
    ones_b = A.t([128, 128], BF16)
    ones_f = A.t([128, 128], F32)
    epst = A.t([128, 1], F32)
    neglam = A.t([128, 1], F32)
    gsc = A.t([128, 1], F32)
    t5c = A.t([128, NT5 * 8, 2], F32)
    A.persist()

    S.add("sp", I("dma_start", out=idt[:, :], in_=c_ident[:, :]), writes=["idt"], dma=True)
    S.add("pool", I("memset", ones_b[:, :], 1.0), writes=["ones_b"])
    S.add("pool", I("memset", ones_f[:, :], 1.0), writes=["ones_f"])
    S.add("pool", I("memset", epst[:, :], 1e-6), writes=["eps"])

    cnt = [0]

    def uid():
        cnt[0] += 1
        return cnt[0]

    def phase0():
        A.reset()
        rp = A.t([31, 368], F32)
        ohc = A.t([31, 4096], F32)
        cm = A.t([1, 4096], F32)
        st = [A.t([128, 512], F32) for _ in range(2)]
        S.add("sp", I("dma_start", out=rp[:, :], in_=rpbT[:, :]), writes=["rp"], dma=True)
        S.add("sp", I("dma_start", out=ohc[:, :], in_=c_ohc[:, :]), writes=["ohc"], dma=True)
        S.add("sp", I("dma_start", out=cm[:, :], in_=c_colmask[:, :]), writes=["cm"], dma=True)
        i = 0
        for m0 in range(0, 368, 128):
            M = min(128, 368 - m0)
            for n in range(8):
                b = i % 2
                i += 1
                S.add("pe", I("matmul", ps[b][0:M, :], lhsT=rp[:, m0:m0 + M], rhs=ohc[:, n * 512:(n + 1) * 512], start=True, stop=False),
                    reads=["rp", "ohc"], writes=[("ps", b)])
                S.add("pe", I("matmul", ps[b][0:M, :], lhsT=ones_f[0:1, 0:M], rhs=cm[0:1, n * 512:(n + 1) * 512], start=False, stop=True),
                    reads=["ones_f", "cm"], writes=[("ps", b)])
                S.add("dve", I("tensor_copy", out=st[b][0:M, :], in_=ps[b][0:M, :]),
                      reads=[("ps", b)], writes=[("st", b)])
                S.add("sp", I("dma_start", out=tfull[m0:m0 + M, n * 512:(n + 1) * 512], in_=st[b][0:M, :]),
                    reads=[("st", b)], writes=["tfull"], dma=True)
        tt = A.t([32, 8], F32)
        tb = A.t([32, 8, 128], F32)
        oh5 = A.t([32, 4 * T5C], F32)
        S.add("sp", I("dma_start", out=tt[:, :], in_=t5t[:, :]), writes=["tt"], dma=True)
        for h in range(8):
            S.add("dve", I("tensor_copy", out=tb[:, h, :], in_=tt[:, h:h + 1].to_broadcast([32, 128])),
                  reads=["tt"], writes=["tb"])
        zst = [A.t([128, T5C], F32) for _ in range(2)]
        j = 0
        for tab in range(NT5):
            s4 = tab % 4
            S.add("sp", I("dma_start", out=oh5[:, s4 * T5C:(s4 + 1) * T5C], in_=c_t5oh[:, tab * T5C:(tab + 1) * T5C]),
                writes=[("oh5", s4)], dma=True)
            for h in range(8):
                zb = j % 2
                j += 1
                for n in range(3):
                    b = i % 2
                    i += 1
                    S.add("pe", I("matmul", ps[b][:, :], lhsT=tb[:, h, :], rhs=oh5[:, s4 * T5C + n * 512: s4 * T5C + (n + 1) * 512],
                        start=True, stop=True), reads=["tb", ("oh5", s4)], writes=[("ps", b)])
                    S.add("dve", I("tensor_copy", out=zst[zb][:, n * 512:(n + 1) * 512], in_=ps[b][:, :]),
                        reads=[("ps", b)], writes=[("zst", zb)])
                S.add("act", I("activation", out=t5c[:, tab * 8 + h, 0:1], in_=zst[zb][:, T5C - 1:T5C], func=AF.Copy),
                    reads=[("zst", zb)], writes=["t5c"])
                S.add("act", I("activation", out=t5c[:, tab * 8 + h, 1:2], in_=zst[zb][:, 1:2], func=AF.Copy),
                    reads=[("zst", zb)], writes=["t5c"])
                S.add("sp", I("dma_start", out=zt5[tab * 8 + h, 0:128, :], in_=zst[zb][:, :]),
                    reads=[("zst", zb)], writes=["zt5"], dma=True)
                S.add("sp", I("dma_start", out=zt5[tab * 8 + h, 128:130, :], in_=zst[zb][0:2, :]),
                    reads=[("zst", zb)], writes=["zt5"], dma=True)
        lv = A.t([1, 256], F32)
        lp = A.t([1, 128], F32)
        l2 = A.t([1, 2], F32)
        sg = A.t([128, 1], F32)
        lam_init = 0.8 - 0.6 * math.exp(-0.3 * 1)
        S.add("sp", I("dma_start", out=lv[:, :], in_=lamv[:, :]), writes=["lv"], dma=True)
        S.add("sp", I("dma_start", out=sg[:, :], in_=subg[:, :]), writes=["sg"], dma=True)
        S.add("dve", I("tensor_tensor", out=lp[:, 0:64], in0=lv[:, 0:64], in1=lv[:, 64:128], op=ALU.mult),
              reads=["lv"], writes=["lp"])
        S.add("dve", I("tensor_tensor", out=lp[:, 64:128], in0=lv[:, 128:192], in1=lv[:, 192:256], op=ALU.mult),
              reads=["lv", "lp"], writes=["lp"])
        S.add("dve", I("reduce_sum", out=l2[:, 0:1], in_=lp[:, 0:64], axis=mybir.AxisListType.X),
              reads=["lp"], writes=["l2"])
        S.add("dve", I("reduce_sum", out=l2[:, 1:2], in_=lp[:, 64:128], axis=mybir.AxisListType.X),
              reads=["lp", "l2"], writes=["l2"])
        S.add("act", I("activation", out=l2[:, :], in_=l2[:, :], func=AF.Exp), reads=["l2"], writes=["l2"])
        S.add("dve", I("scalar_tensor_tensor", out=lp[:, 0:1], in0=l2[:, 1:2], scalar=-lam_init, in1=l2[:, 0:1],
                                                      op0=ALU.add, op1=ALU.subtract),
              reads=["l2", "lp"], writes=["lp"])
        S.add("pe", I("matmul", ps[0][:, 0:1], lhsT=ones_f[0:1, :], rhs=lp[0:1, 0:1], start=True, stop=True),
              reads=["lp", "ones_f"], writes=[("ps", 0)])
        S.add("dve", I("tensor_copy", out=neglam[:, :], in_=ps[0][:, 0:1]), reads=[("ps", 0)], writes=["neglam"])
        S.add("dve", I("tensor_scalar", out=gsc[:, :], in0=sg[:, :], scalar1=1.0 - lam_init, scalar2=None,
                                               op0=ALU.mult), reads=["sg"], writes=["gsc"])

    def load_weight(wb, wap, nchunk, ncol, gcol=None, stg=None, qscale_cols=0, tag="w"):
        for c in range(nchunk):
            sb = c % len(stg)
            S.add("sp", I("dma_start", out=stg[sb][:, 0:ncol], in_=wap[c * 128:(c + 1) * 128, :]),
                  writes=[("wst", sb)], dma=True)
            eng = "dve" if c % 2 == 0 else "pool"
            if gcol is None:
                S.add(eng, I("tensor_copy", out=wb[:, c, :], in_=stg[sb][:, 0:ncol]),
                      reads=[("wst", sb)], writes=[(tag, c)])
            else:
                if qscale_cols:
                    S.add(eng, I("tensor_scalar", out=wb[:, c, 0:qscale_cols], in0=stg[sb][:, 0:qscale_cols], scalar1=gcol[:, c:c + 1],
                        scalar2=0.125, op0=ALU.mult, op1=ALU.mult), reads=[("wst", sb), "gcol"], writes=[(tag, c)])
                S.add(eng, I("tensor_scalar", out=wb[:, c, qscale_cols:ncol], in0=stg[sb][:, qscale_cols:ncol], scalar1=gcol[:, c:c + 1],
                    scalar2=None, op0=ALU.mult), reads=[("wst", sb), "gcol"], writes=[(tag, c)])

    def norm_transpose(xt_ap, xt_id, hb, hb_id, msb, ms_id, hT, hT_id, col0, junk, pst, extra_reads=()):
        S.add("act", I("activation", out=junk[:, :], in_=xt_ap, func=AF.Square, scale=1.0 / 32.0,
                                            accum_out=msb[:, 0:1]),
              reads=[xt_id] + list(extra_reads), writes=["junk", ms_id])
        S.add("act", I("activation", out=msb[:, 1:2], in_=msb[:, 0:1], func=AF.Ln, bias=epst[:, :]),
              reads=[ms_id, "eps"], writes=[ms_id])
        S.add("act", I("activation", out=msb[:, 2:3], in_=msb[:, 1:2], func=AF.Exp, scale=-0.5),
              reads=[ms_id], writes=[ms_id])
        S.add("act", I("activation", out=hb[:, :], in_=xt_ap, func=AF.Copy, scale=msb[:, 2:3]),
              reads=[xt_id, ms_id], writes=[hb_id])
        for c in range(8):
            S.add("pe", I("transpose", out=psb[pst][:, c * 128:(c + 1) * 128],
                                                   in_=hb[:, c * 128:(c + 1) * 128], identity=idt[:, :]),
                  reads=[hb_id, "idt"], writes=[("ps", pst)])
        S.add("dve", I("tensor_copy", out=hT[:, :, col0:col0 + 128], in_=psb[pst][:, :].rearrange("p (c n) -> p c n", c=8)),
            reads=[("ps", pst)], writes=[hT_id])

    def phase_qkv(layer):
        A.reset()
        wb = A.t([128, 8, 3 * D], BF16)
        stg = [A.t([128, 3 * D], F32) for _ in range(2)]
        gcol = A.t([128, 8], F32)
        S.add("sp", I("dma_start", out=gcol[:, :], in_=g_mix_pre[layer, :, :]), writes=["gcol"], dma=True)
        load_weight(wb, w_qkv[layer], 8, 3 * D, gcol=gcol, stg=stg, qscale_cols=D, tag="wqkv")
        xt = [A.t([128, D], F32) for _ in range(3)]
        hb = [A.t([128, D], BF16) for _ in range(2)]
        msb = [A.t([128, 4], F32) for _ in range(2)]
        junk = A.t([128, D], BF16)
        hT = [A.t([128, 8, 512], BF16) for _ in range(2)]
        oqk = [A.t([128, 16, 512], BF16) for _ in range(2)]
        ov = [A.t([128, 4, D], BF16) for _ in range(2)]
        wall = [("wqkv", c) for c in range(8)]
        src = xin if layer == 0 else x2
        it = 0
        evi = 0
        for seg in range(3):
            for g in range((TES[seg] if layer == 0 else T) // 512):
                gs = (seg * 100 + g) % 2
                for j in range(4):
                    it += 1
                    xs, hs = it % 3, it % 2
                    tok0 = g * 512 + j * 128
                    S.add("sp", I("dma_start", out=xt[xs][:, :], in_=src[seg, tok0:tok0 + 128, :]), writes=[("xt", xs)], dma=True)
                    norm_transpose(xt[xs][:, :], ("xt", xs), hb[hs], ("hb", hs), msb[hs], ("ms", hs), hT[gs],
                                   ("hT", gs, j), j * 128, junk, 6 + (it % 2))
                hTall = [("hT", gs, j) for j in range(4)]
                for m in range(16):
                    b = evi % 4
                    for c in range(8):
                        S.add("pe", I("matmul", ps[b][:, :], lhsT=wb[:, c, m * 128:(m + 1) * 128], rhs=hT[gs][:, c, :],
                            start=(c == 0), stop=(c == 7)), reads=[("wqkv", c)] + hTall, writes=[("ps", b)])
                    ee = "dve" if evi % 2 == 0 else "act"
                    evi += 1
                    if ee == "dve":
                        S.add("dve", I("tensor_copy", out=oqk[gs][:, m, :], in_=ps[b][:, :]),
                              reads=[("ps", b)], writes=[("oqk", gs, m)])
                    else:
                        S.add("act", I("activation", out=oqk[gs][:, m, :], in_=ps[b][:, :],
                                                                             func=AF.Copy),
                              reads=[("ps", b)], writes=[("oqk", gs, m)])
                qd = qT0 if layer == 0 else qT1
                S.add("pool", I("dma_start", out=qd[seg, :, :, g * 512:(g + 1) * 512].rearrange("c p n -> p c n"), in_=oqk[gs][:, 0:8, :]),
                    reads=[("oqk", gs, m) for m in range(8)], writes=[("qT", layer, seg)], dma=True)
                if layer == 0:
                    kdst = kT0[seg, :, :, g * 512:(g + 1) * 512].rearrange("c p n -> p c n")
                elif seg < 2:
                    kdst = kT1[seg, :, :, g * 512:(g + 1) * 512].rearrange("c p n -> p c n")
                else:
                    kdst = kT1s[:, g * 512:(g + 1) * 512].rearrange("(c p) n -> p c n", p=128)
                S.add("pool", I("dma_start", out=kdst, in_=oqk[gs][:, 8:16, :]),
                      reads=[("oqk", gs, m) for m in range(8, 16)], writes=[("kT", layer, seg)], dma=True)
                for j in range(4):
                    for hf in range(2):
                        b = evi % 4
                        for c in range(8):
                            S.add("pe", I("matmul", ps[b][:, :], lhsT=hT[gs][:, c, j * 128:(j + 1) * 128],
                                rhs=wb[:, c, 2 * D + hf * 512: 2 * D + (hf + 1) * 512],
                                start=(c == 0), stop=(c == 7)), reads=[("wqkv", c), ("hT", gs, j)], writes=[("ps", b)])
                        ee = "dve" if evi % 2 == 0 else "act"
                        evi += 1
                        if ee == "dve":
                            S.add("dve", I("tensor_copy", out=ov[gs][:, j, hf * 512:(hf + 1) * 512], in_=ps[b][:, :]),
                                reads=[("ps", b)], writes=[("ov", gs, j, hf)])
                        else:
                            S.add("act", I("activation", out=ov[gs][:, j, hf * 512:(hf + 1) * 512], in_=ps[b][:, :], func=AF.Copy),
                                reads=[("ps", b)], writes=[("ov", gs, j, hf)])
                if layer == 0:
                    vdst = v0[seg, g * 512:(g + 1) * 512, :]
                elif seg < 2:
                    vdst = v1[seg, g * 512:(g + 1) * 512, :]
                else:
                    vdst = v1s[g * 512:(g + 1) * 512, :]
                S.add("pool", I("dma_start", out=vdst.rearrange("(j p) n -> p j n", p=128), in_=ov[gs][:, :, :]),
                    reads=[("ov", gs, j, hf) for j in range(4) for hf in range(2)], writes=[("v", layer, seg)],
                    dma=True)

    def phase_na():
        A.reset()
        rmoh = A.t([2, 128], BF16)
        S.add("sp", I("dma_start", out=rmoh[:, :], in_=c_rmoh[:, :]), writes=["rmoh"], dma=True)
        bc = [A.t([128, 16, 512], F32) for _ in range(1)]
        QT = [A.t([128, T], BF16) for _ in range(2)]
        KT = [A.t([128, TE2], BF16) for _ in range(2)]
        VT = [A.t([128, NKT, 128], BF16) for _ in range(2)]
        rmt = [A.t([2, 8 * 512], BF16) for _ in range(2)]
        PT = [A.t([128, 512], BF16) for _ in range(3)]
        rs = [A.t([128, 512], F32) for _ in range(2)]
        ao = [A.t([128, 512], BF16) for _ in range(2)]
        li = 0
        qi = 0
        ui = 0
        for hp in range(8):
            bs = 0
            for hh in range(2):
                h = 2 * hp + hh
                for t in range(8):
                    for a in range(2):
                        drr = 15 - 2 * t - a
                        srcap = bass.AP(tfull, (h * 23 + drr) * 4096, [[64, 64], [4096, 8], [1, 64]])
                        S.add("sp", I("dma_start", out=bc[bs][a * 64:(a + 1) * 64, hh * 8 + t, :].rearrange("p (b c) -> p b c", b=8),
                            in_=srcap), reads=["tfull"], writes=[("bc", bs, hh, t)], dma=True)
            for seg in range(3):
                li += 1
                ls = li % 2
                if seg < 2:
                    S.add("sp", I("dma_start", out=QT[ls][:, :], in_=qT0[seg, hp, :, 256:256 + T]),
                        reads=[("qT", 0, seg)], writes=[("QT", ls)], dma=True)
                else:
                    S.add("sp", I("dma_start", out=QT[ls][:, 0:CH], in_=qT0[seg, hp, :, 256:256 + CH]),
                        reads=[("qT", 0, seg)], writes=[("QT", ls)], dma=True)
                    S.add("sp", I("dma_start", out=QT[ls][:, CH:T], in_=qT0[seg, hp, :, 768 + CH:768 + T]),
                        reads=[("qT", 0, seg), ("QT", ls)], writes=[("QT", ls)], dma=True)
                tes = TES[seg]
                S.add("sp", I("dma_start", out=KT[ls][:, 0:tes], in_=kT0[seg, hp, :, 0:tes]),
                    reads=[("kT", 0, seg)], writes=[("KT", ls)], dma=True)
                S.add("sp", I("dma_start", out=VT[ls][:, 0:tes // 128, :],
                    in_=v0[seg, 0:tes, hp * 128:(hp + 1) * 128].rearrange("(t p) c -> p t c", p=128)),
                    reads=[("v", 0, seg)], writes=[("VT", ls)], dma=True)
                for qt in range(NQT):
                    qi += 1
                    rsl = qi % 2
                    kb0 = 512 * qt + (512 if (seg == 2 and qt >= NQH) else 0)
                    S.add("sp", I("dma_start", out=rmt[rsl][:, :], in_=c_rm[seg, qt, :, :]), writes=[("rmt", rsl)], dma=True)
                    for hh in range(2):
                        po, pS = 4 + hh, 6 + hh
                        pr = slice(64 * hh, 64 * hh + 64)

                        def qk(t, sb):
                            S.add("pe", I("matmul", ps[sb][:, :], lhsT=KT[ls][pr, kb0 + t * 128: kb0 + (t + 1) * 128],
                                rhs=QT[ls][pr, qt * 512:(qt + 1) * 512], start=True, stop=False),
                                reads=[("KT", ls), ("QT", ls)], writes=[("ps", sb)])
                            S.add("pe", I("matmul", ps[sb][:, :], lhsT=rmoh[:, :], rhs=rmt[rsl][:, t * 512:(t + 1) * 512],
                                start=False, stop=True), reads=["rmoh", ("rmt", rsl)], writes=[("ps", sb)])

                        sbs = []
                        for t in range(8):
                            sbs.append((ui + t) % 3)
                        qk(0, sbs[0])
                        for t in range(8):
                            sb = sbs[t]
                            ptb = (ui + t) % 3
                            if t + 1 < 8:
                                qk(t + 1, sbs[t + 1])
                            S.add("dve", I("tensor_tensor", out=ps[sb][:, :], in0=ps[sb][:, :], in1=bc[bs][:, hh * 8 + t, :], op=ALU.add),
                                reads=[("ps", sb), ("bc", bs, hh, t)], writes=[("ps", sb)])
                            S.add("act", I("activation", out=PT[ptb][:, :], in_=ps[sb][:, :], func=AF.Exp),
                                reads=[("ps", sb)], writes=[("PT", ptb)])
                            S.add("pe", I("matmul", ps[po][:, :], lhsT=VT[ls][:, kb0 // 128 + t, :], rhs=PT[ptb][:, :],
                                start=(t == 0), stop=(t == 7)), reads=[("VT", ls), ("PT", ptb)], writes=[("ps", po)])
                            S.add("pe", I("matmul", ps[pS][:, :], lhsT=ones_b[:, :], rhs=PT[ptb][:, :],
                                start=(t == 0), stop=(t == 7)), reads=["ones_b", ("PT", ptb)], writes=[("ps", pS)])
                        ui += 8
                        S.add("dve", I("reciprocal", out=rs[hh][pr, :], in_=ps[pS][pr, :]),
                              reads=[("ps", pS)], writes=[("rs", hh)])
                        S.add("dve", I("tensor_tensor", out=ao[rsl][pr, :], in0=ps[po][pr, :], in1=rs[hh][pr, :], op=ALU.mult),
                            reads=[("ps", po), ("rs", hh)], writes=[("ao", rsl, hh)])
                    S.add("pool", I("dma_start", out=aT[0][seg, hp, :, qt * 512:(qt + 1) * 512], in_=ao[rsl][:, :]),
                        reads=[("ao", rsl, 0), ("ao", rsl, 1)], writes=[("aT", 0, seg)], dma=True)

    def phase_a(layer, xsrc, xsrc_off, xdst):
        A.reset()
        wo = A.t([128, 8, D], BF16)
        wg = A.t([128, 8, DFF], BF16)
        wu = A.t([128, 8, DFF], BF16)
        stg = [A.t([128, DFF], F32) for _ in range(2)]
        gcol = A.t([128, 8], F32)
        gpost = A.t([128, D], F32)
        S.add("sp", I("dma_start", out=gcol[:, :], in_=g_ffn_pre[layer, :, :]), writes=["gcol"], dma=True)
        S.add("sp", I("dma_start", out=gpost[:, :], in_=g_mix_post[layer:layer + 1, :].to_broadcast([128, D])),
              writes=["gpost"], dma=True)
        load_weight(wo, w_o[layer], 8, D, stg=stg, tag="wo")
        load_weight(wg, w_gate[layer], 8, DFF, gcol=gcol, stg=stg, tag="wg")
        load_weight(wu, w_up[layer], 8, DFF, gcol=gcol, stg=stg, tag="wu")
        at = [A.t([128, 8, 512], BF16) for _ in range(2)]
        xt = [A.t([128, D], F32) for _ in range(2)]
        xm = [A.t([128, D], F32) for _ in range(2)]
        hb = [A.t([128, D], BF16) for _ in range(2)]
        msb = [A.t([128, 8], F32) for _ in range(2)]
        junk = A.t([128, D], BF16)
        hT = [A.t([128, 8, 512], BF16) for _ in range(2)]
        sg = [A.t([128, 512], F32) for _ in range(2)]
        oa = [A.t([128, 2, 512], BF16) for _ in range(2)]
        it = 0
        gi = 0
        ci = 0
        for seg in range(3):
            for g in range(T // 512):
                gi += 1
                gs = gi % 2
                S.add("sp", I("dma_start", out=at[gs][:, :, :], in_=aT[layer][seg, :, :, g * 512:(g + 1) * 512].rearrange("c p n -> p c n")),
                    reads=[("aT", layer, seg)], writes=[("at", gs)], dma=True)
                for j in range(4):
                    it += 1
                    s2 = it % 2
                    tok0 = g * 512 + j * 128
                    xo_ = xsrc_off + tok0 + (512 if (xsrc_off and seg == 2 and tok0 >= CH) else 0)
                    S.add("sp", I("dma_start", out=xt[s2][:, :], in_=xsrc[seg, xo_: xo_ + 128, :]),
                        reads=[("xsrc", seg)], writes=[("xt", s2)], dma=True)
                    pa, pb_ = (0, 1) if s2 == 0 else (2, 3)
                    for hf, pbk in ((0, pa), (1, pb_)):
                        for c in range(8):
                            S.add("pe", I("matmul", ps[pbk][:, :], lhsT=at[gs][:, c, j * 128:(j + 1) * 128],
                                rhs=wo[:, c, hf * 512:(hf + 1) * 512], start=(c == 0), stop=(c == 7)),
                                reads=[("at", gs), ("wo", c)], writes=[("ps", pbk)])
                    m = msb[s2]
                    mid = ("ms", s2)
                    for hf, pbk in ((0, pa), (1, pb_)):
                        S.add("act", I("activation", out=junk[:, 0:512], in_=ps[pbk][:, :], func=AF.Square, scale=1.0 / 32.0,
                            accum_out=m[:, hf:hf + 1]), reads=[("ps", pbk)], writes=["junk", mid])
                    S.add("dve", I("tensor_tensor", out=m[:, 2:3], in0=m[:, 0:1], in1=m[:, 1:2], op=ALU.add),
                          reads=[mid], writes=[mid])
                    S.add("act", I("activation", out=m[:, 3:4], in_=m[:, 2:3], func=AF.Ln, bias=epst[:, :]),
                          reads=[mid, "eps"], writes=[mid])
                    S.add("act", I("activation", out=m[:, 4:5], in_=m[:, 3:4], func=AF.Exp, scale=-0.5),
                          reads=[mid], writes=[mid])
                    for hf, pbk in ((0, pa), (1, pb_)):
                        S.add("dve", I("scalar_tensor_tensor", out=xm[s2][:, hf * 512:(hf + 1) * 512], in0=ps[pbk][:, :], scalar=m[:, 4:5],
                            in1=gpost[:, hf * 512:(hf + 1) * 512], op0=ALU.mult, op1=ALU.mult),
                            reads=[("ps", pbk), mid, "gpost"], writes=[("xm", s2, hf)])
                    S.add("pool", I("tensor_tensor", out=xm[s2][:, :], in0=xm[s2][:, :], in1=xt[s2][:, :],
                                                                   op=ALU.add),
                          reads=[("xm", s2, 0), ("xm", s2, 1), ("xt", s2)], writes=[("xm", s2, 0), ("xm", s2, 1)])
                    S.add("pool", I("dma_start", out=xdst[seg, tok0:tok0 + 128, :], in_=xm[s2][:, :]),
                        reads=[("xm", s2, 0), ("xm", s2, 1)], writes=[("xdst", seg)], dma=True)
                    m2 = msb[s2]
                    S.add("act", I("activation", out=junk[:, :], in_=xm[s2][:, :], func=AF.Square, scale=1.0 / 32.0, accum_out=m2[:, 5:6]),
                        reads=[("xm", s2, 0), ("xm", s2, 1)], writes=["junk", mid])
                    S.add("act", I("activation", out=m2[:, 6:7], in_=m2[:, 5:6], func=AF.Ln,
                                                               bias=epst[:, :]), reads=[mid, "eps"], writes=[mid])
                    S.add("act", I("activation", out=m2[:, 7:8], in_=m2[:, 6:7], func=AF.Exp, scale=-0.5),
                          reads=[mid], writes=[mid])
                    S.add("act", I("activation", out=hb[s2][:, :], in_=xm[s2][:, :], func=AF.Copy,
                                                                      scale=m2[:, 7:8]),
                          reads=[("xm", s2, 0), ("xm", s2, 1), mid], writes=[("hb", s2)])
                    pst = 4 + s2
                    for c in range(8):
                        S.add("pe", I("transpose", out=psb[pst][:, c * 128:(c + 1) * 128], in_=hb[s2][:, c * 128:(c + 1) * 128],
                            identity=idt[:, :]), reads=[("hb", s2), "idt"], writes=[("ps", pst)])
                    S.add("dve", I("tensor_copy", out=hT[gs][:, :, j * 128:(j + 1) * 128],
                        in_=psb[pst][:, :].rearrange("p (c n) -> p c n", c=8)),
                        reads=[("ps", pst)], writes=[("hT", gs, j)])
                hTall = [("hT", gs, j) for j in range(4)]
                for fc in range(NFC):
                    ci += 1
                    pg, pu = (6, 7) if fc % 2 == 0 else (4, 5)
                    for c in range(8):
                        S.add("pe", I("matmul", ps[pg][:, :], lhsT=wg[:, c, fc * 128:(fc + 1) * 128], rhs=hT[gs][:, c, :],
                            start=(c == 0), stop=(c == 7)), reads=[("wg", c)] + hTall, writes=[("ps", pg)])
                    for c in range(8):
                        S.add("pe", I("matmul", ps[pu][:, :], lhsT=wu[:, c, fc * 128:(fc + 1) * 128], rhs=hT[gs][:, c, :],
                            start=(c == 0), stop=(c == 7)), reads=[("wu", c)] + hTall, writes=[("ps", pu)])
                    sgb = ci % 2
                    os_ = ((ci - 1) // 2) % 2
                    S.add("act", I("activation", out=sg[sgb][:, :], in_=ps[pg][:, :], func=AF.Exp, scale=-1.0),
                          reads=[("ps", pg)], writes=[("sg", sgb)])
                    S.add("act", I("activation", out=sg[sgb][:, :], in_=sg[sgb][:, :], func=AF.Ln, bias=ones_f[:, 0:1]),
                          reads=[("sg", sgb), "ones_f"], writes=[("sg", sgb)])
                    S.add("act", I("activation", out=sg[sgb][:, :], in_=sg[sgb][:, :], func=AF.Exp, scale=-1.0),
                          reads=[("sg", sgb)], writes=[("sg", sgb)])
                    S.add("dve", I("tensor_tensor", out=sg[sgb][:, :], in0=ps[pg][:, :], in1=sg[sgb][:, :], op=ALU.mult),
                          reads=[("ps", pg), ("sg", sgb)], writes=[("sg", sgb)])
                    S.add("dve", I("tensor_tensor", out=oa[os_][:, fc % 2, :], in0=ps[pu][:, :], in1=sg[sgb][:, :], op=ALU.mult),
                        reads=[("ps", pu), ("sg", sgb)], writes=[("oa", os_, fc % 2)])
                    if fc % 2 == 1:
                        S.add("pool", I("dma_start", out=actT[seg, fc - 1:fc + 1, :, g * 512:(g + 1) * 512].rearrange("c p n -> p c n"),
                            in_=oa[os_][:, :, :]), reads=[("oa", os_, 0), ("oa", os_, 1)],
                            writes=[("actT", seg)], dma=True)

    def phase_b(layer, xmid, xdst):
        A.reset()
        wd = A.t([128, NFC, D], BF16)
        stg = [A.t([128, D], F32) for _ in range(3)]
        gpost = A.t([128, D], F32)
        S.add("sp", I("dma_start", out=gpost[:, :], in_=g_ffn_post[layer:layer + 1, :].to_broadcast([128, D])),
              writes=["gpost"], dma=True)
        load_weight(wd, w_down[layer], NFC, D, stg=stg, tag="wd")
        wdall = [("wd", c) for c in range(NFC)]
        at = [A.t([128, NFC, 512], BF16) for _ in range(2)]
        xt = [A.t([128, D], F32) for _ in range(3)]
        xo = [A.t([128, D], F32) for _ in range(2)]
        msb = [A.t([128, 8], F32) for _ in range(2)]
        junk = A.t([128, 512], BF16)
        it = 0
        gi = 0
        for seg in range(3):
            for g in range(T // 512):
                gi += 1
                gs = gi % 2
                for hf in range(2):
                    S.add("sp", I("dma_start", out=at[gs][:, hf * 11:(hf + 1) * 11, :],
                        in_=actT[seg, hf * 11:(hf + 1) * 11, :, g * 512:(g + 1) * 512].rearrange("c p n -> p c n")),
                        reads=[("actT", seg)], writes=[("at", gs, hf)], dma=True)
                for j in range(4):
                    it += 1
                    s2, s3 = it % 2, it % 3
                    tok0 = g * 512 + j * 128
                    S.add("sp", I("dma_start", out=xt[s3][:, :], in_=xmid[seg, tok0:tok0 + 128, :]),
                        reads=[("xdst", seg)], writes=[("xt", s3)], dma=True)
                    pa, pb_ = (0, 1) if s2 == 0 else (2, 3)
                    for hf, pbk in ((0, pa), (1, pb_)):
                        for c in range(NFC):
                            S.add("pe", I("matmul", ps[pbk][:, :], lhsT=at[gs][:, c, j * 128:(j + 1) * 128],
                                rhs=wd[:, c, hf * 512:(hf + 1) * 512], start=(c == 0), stop=(c == NFC - 1)),
                                reads=[("at", gs, 0), ("at", gs, 1), ("wd", c)], writes=[("ps", pbk)])
                    m = msb[s2]
                    mid = ("ms", s2)
                    for hf, pbk in ((0, pa), (1, pb_)):
                        S.add("act", I("activation", out=junk[:, 0:512], in_=ps[pbk][:, :], func=AF.Square, scale=1.0 / 32.0,
                            accum_out=m[:, hf:hf + 1]), reads=[("ps", pbk)], writes=["junk", mid])
                    S.add("dve", I("tensor_tensor", out=m[:, 2:3], in0=m[:, 0:1], in1=m[:, 1:2], op=ALU.add),
                          reads=[mid], writes=[mid])
                    S.add("act", I("activation", out=m[:, 3:4], in_=m[:, 2:3], func=AF.Ln, bias=epst[:, :]),
                          reads=[mid, "eps"], writes=[mid])
                    S.add("act", I("activation", out=m[:, 4:5], in_=m[:, 3:4], func=AF.Exp, scale=-0.5),
                          reads=[mid], writes=[mid])
                    for hf, pbk in ((0, pa), (1, pb_)):
                        S.add("dve", I("scalar_tensor_tensor", out=xo[s2][:, hf * 512:(hf + 1) * 512], in0=ps[pbk][:, :], scalar=m[:, 4:5],
                            in1=gpost[:, hf * 512:(hf + 1) * 512], op0=ALU.mult, op1=ALU.mult),
                            reads=[("ps", pbk), mid, "gpost"], writes=[("xo", s2, hf)])
                    S.add("pool", I("tensor_tensor", out=xo[s2][:, :], in0=xo[s2][:, :], in1=xt[s3][:, :], op=ALU.add),
                        reads=[("xo", s2, 0), ("xo", s2, 1), ("xt", s3)], writes=[("xo", s2, 0), ("xo", s2, 1)])
                    S.add("pool", I("dma_start", out=xdst[seg, tok0:tok0 + 128, :], in_=xo[s2][:, :]),
                        reads=[("xo", s2, 0), ("xo", s2, 1)], writes=[("xnext", seg)], dma=True)

    def phase_diff():
        A.reset()
        NKB = 12
        NKT_C = CH // 128
        QT = [A.t([128, T], BF16) for _ in range(2)]
        KB = [A.t([128, CH], BF16) for _ in range(NKB)]
        VB = [A.t([128, NKT_C, 128], BF16) for _ in range(NKB)]
        bt = [A.t([128, 8, 512], F32) for _ in range(2)]
        PT = [A.t([128, 512], BF16) for _ in range(4)]
        tmp = [A.t([128, 512], F32) for _ in range(6)]
        ao = [A.t([128, 512], BF16) for _ in range(2)]
        kbi = 0
        hi = 0
        ui = 0
        qi = 0
        for seg in range(3):
            for h in range(8):
                hi += 1
                hs = hi % 2
                S.add("sp", I("dma_start", out=QT[hs][:, :], in_=qT1[seg, h, :, :]),
                      reads=[("qT", 1, seg)], writes=[("QT", hs)], dma=True)

                def load_bt(tab, slot, dlt, bsl):
                    off = (tab * 8 + h) * 130 * T5C + (T5C0 - dlt)
                    srcap = bass.AP(zt5, off, [[T5C - 1, 128], [1, 512]])
                    S.add("sp", I("dma_start", out=bt[bsl][:, slot, :], in_=srcap), reads=["zt5"], writes=[("bt", bsl, slot)], dma=True)

                for part in range(1 if seg < 2 else 2):
                    nch = 2 if seg < 2 else 8
                    chunks = []
                    for r in range(nch):
                        kb = kbi % NKB
                        kbi += 1
                        if seg < 2:
                            ksrc = kT1[seg, h, :, r * CH:(r + 1) * CH]
                            vsrc = v1[seg, r * CH:(r + 1) * CH, h * 128:(h + 1) * 128]
                            kdep, vdep = ("kT", 1, seg), ("v", 1, seg)
                        else:
                            ksrc = kTg[r * 1024 + h * 128: r * 1024 + (h + 1) * 128, part * CH:(part + 1) * CH]
                            vsrc = vg[r * T + part * CH: r * T + (part + 1) * CH, h * 128:(h + 1) * 128]
                            kdep, vdep = "kTg", "vg"
                        S.add("sp", I("dma_start", out=KB[kb][:, :], in_=ksrc),
                              reads=[kdep], writes=[("KB", kb)], dma=True)
                        S.add("sp", I("dma_start", out=VB[kb][:, :, :], in_=vsrc.rearrange("(t p) c -> p t c", p=128)),
                            reads=[vdep], writes=[("VB", kb)], dma=True)
                        chunks.append(kb)
                    qts = list(range(NQT)) if seg < 2 else list(range(part * NQH, (part + 1) * NQH))
                    for qt in qts:
                        qi += 1
                        O1, O2, S1, S2 = 4, 5, 6, 7
                        units = []
                        for r in range(nch):
                            for ktl in range(NKT_C):
                                if seg < 2:
                                    d = r * NKT_C + ktl - 4 * qt
                                    if -1 <= d <= 4:
                                        units.append((r, ktl, "tile", (0, d + 1, d * 128)))
                                    else:
                                        units.append((r, ktl, "const", (0, 0 if d < -1 else 1)))
                                else:
                                    qtl = qt - part * NQH
                                    d = ktl - 4 * qtl
                                    if -1 <= d <= 4:
                                        units.append((r, ktl, "tile", (1 + r, d + 1, d * 128)))
                                    elif qtl == 0 and ktl == NKT_C - 1:
                                        units.append((r, ktl, "tile", (9 + r, 6, -128)))
                                    elif qtl == NQH - 1 and ktl == 0:
                                        units.append((r, ktl, "tile", (17 + r, 7, 512)))
                                    else:
                                        units.append((r, ktl, "const", (1 + r, 0 if d < -1 else 1)))
                        nU = len(units)

                        def qk(u, sbp):
                            r, ktl, mode, arg = units[u]
                            kb = chunks[r]
                            for mp in range(2):
                                prr = slice(64 * mp, 64 * mp + 64)
                                S.add("pe", I("matmul", ps[sbp[mp]][:, :], lhsT=KB[kb][prr, ktl * 128:(ktl + 1) * 128],
                                    rhs=QT[hs][prr, qt * 512:(qt + 1) * 512], start=True, stop=True),
                                    reads=[("KB", kb), ("QT", hs)], writes=[("ps", sbp[mp])])

                        def sbpair(u):
                            return (0, 1) if (ui + u) % 2 == 0 else (2, 3)

                        qk(0, sbpair(0))
                        for u in range(nU):
                            r, ktl, mode, arg = units[u]
                            kb = chunks[r]
                            sbp = sbpair(u)
                            if u + 1 < nU:
                                qk(u + 1, sbpair(u + 1))
                            if mode == "tile":
                                tab, slot, dlt = arg
                                bsl = (ui + u) % 2
                                load_bt(tab, slot, dlt, bsl)
                                for mp in range(2):
                                    S.add("dve", I("tensor_tensor", out=ps[sbp[mp]][:, :], in0=ps[sbp[mp]][:, :], in1=bt[bsl][:, slot, :], op=ALU.add),
                                        reads=[("ps", sbp[mp]), ("bt", bsl, slot)], writes=[("ps", sbp[mp])])
                                biasap = None
                            else:
                                tab, side = arg
                                biasap = t5c[:, tab * 8 + h, side:side + 1]
                            for mp in range(2):
                                ptb = 2 * ((ui + u) % 2) + mp
                                if biasap is None:
                                    S.add("act", I("activation", out=PT[ptb][:, :], in_=ps[sbp[mp]][:, :], func=AF.Exp),
                                        reads=[("ps", sbp[mp])], writes=[("PT", ptb)])
                                else:
                                    S.add("act", I("activation", out=PT[ptb][:, :], in_=ps[sbp[mp]][:, :], func=AF.Exp, bias=biasap),
                                        reads=[("ps", sbp[mp]), "t5c"], writes=[("PT", ptb)])
                                S.add("pe", I("matmul", ps[O1 + mp][:, :], lhsT=VB[kb][:, ktl, :], rhs=PT[ptb][:, :],
                                    start=(u == 0), stop=(u == nU - 1)),
                                    reads=[("VB", kb), ("PT", ptb)], writes=[("ps", O1 + mp)])
                                S.add("pe", I("matmul", ps[S1 + mp][:, :], lhsT=ones_b[:, :], rhs=PT[ptb][:, :],
                                    start=(u == 0), stop=(u == nU - 1)),
                                    reads=["ones_b", ("PT", ptb)], writes=[("ps", S1 + mp)])
                        ui += nU
                        r1, r2, aa, bb, uu, sq = tmp
                        S.add("dve", I("reciprocal", out=r1[:, :], in_=ps[S1][:, :]), reads=[("ps", S1)],
                              writes=["r1"])
                        S.add("dve", I("reciprocal", out=r2[:, :], in_=ps[S2][:, :]), reads=[("ps", S2)],
                              writes=["r2"])
                        S.add("dve", I("tensor_tensor", out=aa[:, :], in0=ps[O1][:, :], in1=r1[:, :], op=ALU.mult),
                              reads=[("ps", O1), "r1"], writes=["aa"])
                        S.add("dve", I("tensor_tensor", out=bb[:, :], in0=ps[O2][:, :], in1=r2[:, :], op=ALU.mult),
                              reads=[("ps", O2), "r2"], writes=["bb"])
                        S.add("dve", I("scalar_tensor_tensor", out=uu[:, :], in0=bb[:, :], scalar=neglam[:, 0:1], in1=aa[:, :], op0=ALU.mult, op1=ALU.add),
                            reads=["aa", "bb", "neglam"], writes=["uu"])
                        S.add("pool", I("tensor_tensor", out=sq[:, :], in0=uu[:, :], in1=uu[:, :], op=ALU.mult),
                              reads=["uu"], writes=["sq"])
                        S.add("pe", I("matmul", ps[S1][:, :], lhsT=ones_f[:, :], rhs=sq[:, :], start=True, stop=True),
                              reads=["ones_f", "sq"], writes=[("ps", S1)])
                        S.add("act", I("activation", out=r1[:, :], in_=ps[S1][:, :], func=AF.Ln, scale=1.0 / 128.0,
                                                            bias=epst[:, :]), reads=[("ps", S1), "eps"], writes=["r1"])
                        S.add("act", I("activation", out=r2[:, :], in_=r1[:, :], func=AF.Exp, scale=-0.5),
                              reads=["r1"], writes=["r2"])
                        aos = qi % 2
                        S.add("dve", I("scalar_tensor_tensor", out=ao[aos][:, :], in0=uu[:, :], scalar=gsc[:, 0:1], in1=r2[:, :], op0=ALU.mult, op1=ALU.mult),
                            reads=["uu", "r2", "gsc"], writes=[("ao", aos)])
                        S.add("pool", I("dma_start", out=aT[1][seg, h, :, qt * 512:(qt + 1) * 512], in_=ao[aos][:, :]),
                            reads=[("ao", aos)], writes=[("aT", 1, seg)], dma=True)

    def phase_gather():
        rg = [list(range(8))]
        S.add("pool", I("collective_compute", "AllGather", ALU.bypass, replica_groups=rg, ins=[kT1s[:, :]], outs=[kTg[:, :]]),
            reads=[("kT", 1, 2)], writes=["kTg"], dma=True, inc=1)
        S.add("pool", I("collective_compute", "AllGather", ALU.bypass, replica_groups=rg, ins=[v1s[:, :]], outs=[vg[:, :]]),
            reads=[("v", 1, 2)], writes=["vg"], dma=True, inc=1)

    phases = [
        ("p0", phase0),
        ("p1", lambda: phase_qkv(0)),
        ("p2", phase_na),
        ("p3a", lambda: phase_a(0, xin, 256, x1)),
        ("p3b", lambda: phase_b(0, x1, x2)),
        ("p4", lambda: phase_qkv(1)),
        ("pg", phase_gather if use_cc else (lambda: None)),
        ("p5", phase_diff),
        ("p6a", lambda: phase_a(1, x2, 0, x3)),
        ("p6b", lambda: phase_b(1, x3, yout)),
    ]
    for name, fn in phases:
        fn()
        S.barrier()
        if stop_after == name:
            break
    S.emit()
    return nc


def make_in_maps(inp, rows):
    T = rows * 64
    TE2 = T + 1024
    CH = T // 2
    xp = np.asarray(inp["x_prompt"], np.float32)
    xs = np.asarray(inp["x_sample"], np.float32)
    common = {
        "na_w_qkv": np.ascontiguousarray(inp["na_w_qkv"][0]), "diff_w_qkv": np.ascontiguousarray(inp["diff_w_qkv"][0]),
        "na_w_o": np.ascontiguousarray(inp["na_w_o"][0]), "diff_w_o": np.ascontiguousarray(inp["diff_w_o"][0]),
        "ffn_w_gate": np.ascontiguousarray(inp["ffn_w_gate"]), "ffn_w_up": np.ascontiguousarray(inp["ffn_w_up"]),
        "ffn_w_down": np.ascontiguousarray(inp["ffn_w_down"]),
        "g_mix_pre": np.stack([_gcol(np.asarray(inp["mix_pre_g"][i])) for i in range(2)]),
        "g_ffn_pre": np.stack([_gcol(np.asarray(inp["ffn_pre_g"][i])) for i in range(2)]),
        "g_mix_post": np.ascontiguousarray(inp["mix_post_g"]), "g_ffn_post": np.ascontiguousarray(inp["ffn_post_g"]),
        "subg": np.ascontiguousarray(np.asarray(inp["diff_subln_g"][0]).reshape(128, 1)),
        "t5t": np.ascontiguousarray(inp["t5_bias"]),
        "lamv": np.concatenate([np.asarray(inp[k][0]) for k in
                                ("diff_lambda_q1", "diff_lambda_k1", "diff_lambda_q2", "diff_lambda_k2")]).reshape(1, 256),
    }
    rpb = np.asarray(inp["na_rpb"][0], np.float32)
    rp = np.zeros((31, 16, 23), np.float32)
    for drr in range(4, 19):
        rp[:, :, drr] = rpb[:, 18 - drr, :].T
    common["rpbT"] = rp.reshape(31, 368)
    common = {k: np.ascontiguousarray(v, dtype=np.float32) for k, v in common.items()}
    maps = []
    for core in range(NCORES):
        m = dict(common)
        m.update(_static_tables(rows, core))
        x = np.zeros((3, TE2, D), np.float32)
        x[0, 256:256 + T] = xp[2 * core]
        x[1, 256:256 + T] = xp[2 * core + 1]
        for sq in range(2):
            lo, hi = core * CH - 256, core * CH + CH + 256
            a, b = max(lo, 0), min(hi, 4 * T)
            base = sq * (CH + 512)
            x[2, base + a - lo: base + a - lo + (b - a)] = xs[sq, a:b]
        m["xin"] = x
        maps.append(m)
    return maps


def run(inp, rows, debug_outs=(), use_cc=True, stop_after=None, trace=False):
    nc = build_program(rows, debug_outs=debug_outs, use_cc=use_cc, stop_after=stop_after)
    maps = make_in_maps(inp, rows)
    res = run_bass_kernel_spmd(nc, maps, core_ids=list(range(NCORES)), **({"trace": True} if trace else {}))
    return res


def kernel(**inputs):
    rows = 64
    T = rows * 64
    res = run(inputs, rows)
    yp = np.zeros((16, T, D), np.float32)
    ys = np.zeros((2, 4 * T, D), np.float32)
    for core in range(NCORES):
        y = res.results[core]["yout"]
        yp[2 * core] = y[0]
        yp[2 * core + 1] = y[1]
        ys[0, core * (T // 2):(core + 1) * (T // 2)] = y[2][:T // 2]
        ys[1, core * (T // 2):(core + 1) * (T // 2)] = y[2][T // 2:]
    return yp, ys
```

```python
import math
import numpy as np
import ml_dtypes
import concourse.bass as bass
import concourse.mybir as mybir
from concourse.bass_utils import run_bass_kernel_spmd

F32 = mybir.dt.float32
BF16 = mybir.dt.bfloat16
ALU = mybir.AluOpType
AF = mybir.ActivationFunctionType
BF = ml_dtypes.bfloat16

D = 1024
DFF = 2816
NFC = DFF // 128
NEG = -1e30
NCORES = 8


class _Op:
    __slots__ = ("eng", "fn", "tl", "pos", "waits", "flag", "vc", "dma", "count", "inc")


class Sched:
    ENGS = ("pe", "act", "dve", "pool", "sp")

    def __init__(self, nc, ndma=None):
        self.nc = nc
        ndma = ndma or {"sp": 16, "pool": 8, "act": 4}
        self.ops = {e: [] for e in self.ENGS}
        self.tl_ops = {e: [] for e in ("pe", "act", "dve", "pool")}
        self.dma_tls = {}
        for q, n in ndma.items():
            self.dma_tls[q] = [f"d_{q}{i}" for i in range(n)]
            for t in self.dma_tls[q]:
                self.tl_ops[t] = []
        self.dma_rr = {q: 0 for q in ndma}
        self.clock = {e: {} for e in self.ENGS}
        self.lastw = {}
        self.readers = {}

    def _need(self, eng, X, Y, raw):
        if Y is None or Y is X:
            return
        if (not Y.dma) and Y.eng == eng and not X.dma:
            if eng == "pe" or not raw:
                return
        ck = self.clock[eng]
        if ck.get(Y.tl, 0) >= Y.pos:
            return
        Y.flag = True
        ck = dict(ck)
        for t, p in Y.vc.items():
            if ck.get(t, 0) < p:
                ck[t] = p
        if ck.get(Y.tl, 0) < Y.pos:
            ck[Y.tl] = Y.pos
        self.clock[eng] = ck
        if X.waits.get(Y.tl, 0) < Y.pos:
            X.waits[Y.tl] = Y.pos

    def add(self, eng, fn, reads=(), writes=(), dma=False, inc=16):
        X = _Op()
        X.eng, X.fn, X.dma, X.flag, X.waits = eng, fn, dma, False, {}
        X.inc = inc
        if dma:
            tls = self.dma_tls[eng]
            X.tl = tls[self.dma_rr[eng] % len(tls)]
            self.dma_rr[eng] += 1
        else:
            X.tl = eng
        lst = self.tl_ops[X.tl]
        X.pos = len(lst) + 1
        if dma and lst:
            self._need(eng, X, lst[-1], True)
        for b in reads:
            self._need(eng, X, self.lastw.get(b), True)
        for b in writes:
            self._need(eng, X, self.lastw.get(b), False)
            rd = self.readers.get(b)
            if rd:
                for Y in rd.values():
                    self._need(eng, X, Y, False)
        X.vc = self.clock[eng]
        lst.append(X)
        self.ops[eng].append(X)
        for b in writes:
            self.lastw[b] = X
            self.readers[b] = {}
        for b in reads:
            self.readers.setdefault(b, {})[X.tl] = X
        return X

    def barrier(self, dma_only_for=("sp",)):
        last = [lst[-1] for lst in self.tl_ops.values() if lst]
        for eng in self.ENGS:
            X = _Op()
            X.eng, X.fn, X.dma, X.flag, X.waits = eng, None, False, False, {}
            X.tl, X.pos = None, 0
            for Y in last:
                if (not Y.dma) and Y.eng == eng:
                    continue
                self._need(eng, X, Y, True)
            X.vc = self.clock[eng]
            self.ops[eng].append(X)
        self.lastw = {}
        self.readers = {}

    def emit(self):
        nc = self.nc
        sems = {}
        for t, lst in self.tl_ops.items():
            c = 0
            for op in lst:
                if op.dma:
                    c += op.inc
                elif op.flag:
                    c += 1
                op.count = c
            if lst:
                sems[t] = nc.alloc_semaphore(f"s_{t}")
        tl_ops = self.tl_ops

        def run(e, name):
            for op in self.ops[name]:
                for t, p in op.waits.items():
                    e.wait_ge(sems[t], tl_ops[t][p - 1].count)
                if op.fn is None:
                    continue
                inst = op.fn(e)
                if op.dma:
                    inst.then_inc(sems[op.tl], op.inc)
                elif op.flag:
                    inst.then_inc(sems[op.tl], 1)

        with nc.Block() as block:
            @block.tensor
            def _(e):
                run(e, "pe")

            @block.scalar
            def _(e):
                run(e, "act")

            @block.vector
            def _(e):
                run(e, "dve")

            @block.gpsimd
            def _(e):
                run(e, "pool")

            @block.sync
            def _(e):
                run(e, "sp")


def I(name, *args, **kw):
    return lambda e: getattr(e, name)(*args, **kw)

class Arena:
    LO, HI = 16512, 229300

    def __init__(self, nc):
        self.nc, self.off, self.n, self.base = nc, self.LO, 0, self.LO

    def t(self, shape, dtype):
        sz = 4 if dtype == F32 else 2
        nb = sz
        for s in shape[1:]:
            nb *= s
        nb = (nb + 63) // 64 * 64
        assert self.off + nb <= self.HI, f"SBUF overflow {self.off + nb}"
        self.n += 1
        h = self.nc.alloc_sbuf_tensor_at(f"sb{self.n}", list(shape), dtype, offset=self.off)
        self.off += nb
        return h

    def persist(self):
        self.base = self.off

    def reset(self):
        self.off = self.base


def _t5_bucket(rel):
    nb, me = 16, 8
    ret = np.where(rel > 0, nb, 0)
    n = np.abs(rel)
    nf = np.maximum(n, me).astype(np.float32)
    large = me + (np.log(nf / me) / np.float32(math.log(128 / me)) * (nb - me)).astype(np.int32)
    large = np.minimum(large, nb - 1)
    return ret + np.where(n < me, n, large)


T5C0, T5C = 768, 1536
NT5 = 25


def _static_tables(rows, core):
    T = rows * 64
    nqt = T // 512
    sl = core
    out = {}
    kc = np.arange(64)[:, None]
    c = np.arange(64)[None, :]
    dc = np.clip(kc - c + 15, 0, 30)
    oh = np.zeros((31, 64, 64), np.float32)
    for i in range(31):
        oh[i] = (dc == i)
    out["c_ohc"] = oh.reshape(31, 4096)
    qs = np.clip(c - 8, 0, 48)
    out["c_colmask"] = np.where((kc >= qs) & (kc < qs + 16), 0.0, NEG).astype(np.float32).reshape(1, 4096)
    rm = np.zeros((3, nqt, 2, 8, 8, 64), np.float32)
    for seg in range(3):
        rows_seq = rows if seg < 2 else 4 * rows
        for qt in range(nqt):
            if seg < 2:
                q0 = 8 * qt
            else:
                q0 = sl * (rows // 2) + 8 * (qt % (nqt // 2))
            for t in range(8):
                for a in range(2):
                    kr = q0 + 2 * t + a - 4
                    for b in range(8):
                        r = q0 + b
                        row0 = min(max(r - 4, 0), rows_seq - 8)
                        ok = (0 <= kr < rows_seq) and (row0 <= kr < row0 + 8)
                        rm[seg, qt, a, t, b, :] = 0.0 if ok else NEG
    out["c_rm"] = rm.reshape(3, nqt, 2, 8 * 512).astype(BF)
    ohr = np.zeros((2, 128), np.float32)
    ohr[0, :64] = 1
    ohr[1, 64:] = 1
    out["c_rmoh"] = ohr.astype(BF)
    cidx = np.arange(T5C)
    bt = _t5_bucket(T5C0 - cidx)
    true = np.zeros((32, T5C), np.float32)
    true[bt, cidx] = 1
    L = np.zeros((32, T5C), np.float32)
    L[15] = 1
    R = np.zeros((32, T5C), np.float32)
    R[31] = 1
    tabs = np.zeros((32, NT5, T5C), np.float32)
    tabs[:, 0] = true
    for r in range(8):
        tabs[:, 1 + r] = true if r == sl else (L if r < sl else R)
        tabs[:, 9 + r] = true if r == sl - 1 else (L if r < sl - 1 else R)
        tabs[:, 17 + r] = true if r == sl + 1 else (L if r <= sl else R)
    out["c_t5oh"] = tabs.reshape(32, NT5 * T5C)
    out["c_ident"] = np.eye(128, dtype=np.float32).astype(BF)
    return out


def _gcol(g):
    return np.ascontiguousarray(g.reshape(-1, 128).T)


def build_program(rows=64, debug_outs=(), use_cc=True, stop_after=None):
    T = rows * 64
    TE = T + 512
    TE2 = T + 1024
    TES = [TE, TE, TE2]
    CH = T // 2
    NQT = T // 512
    NQH = NQT // 2
    NKT = TE2 // 128
    nc = bass.Bass("TRN2", target_bir_lowering=False)
    S = Sched(nc)
    A = Arena(nc)

    def din(name, shape, dt=F32):
        return nc.dram_tensor(name, list(shape), dt, kind="ExternalInput")

    def dscr(name, shape, dt):
        kind = "ExternalOutput" if name in debug_outs else "Internal"
        return nc.dram_tensor(name, list(shape), dt, kind=kind)

    xin = din("xin", [3, TE2, D])
    w_qkv = [din("na_w_qkv", [D, 3 * D]), din("diff_w_qkv", [D, 3 * D])]
    w_o = [din("na_w_o", [D, D]), din("diff_w_o", [D, D])]
    w_gate = din("ffn_w_gate", [2, D, DFF])
    w_up = din("ffn_w_up", [2, D, DFF])
    w_down = din("ffn_w_down", [2, DFF, D])
    g_mix_pre = din("g_mix_pre", [2, 128, 8])
    g_ffn_pre = din("g_ffn_pre", [2, 128, 8])
    g_mix_post = din("g_mix_post", [2, D])
    g_ffn_post = din("g_ffn_post", [2, D])
    rpbT = din("rpbT", [31, 368])
    lamv = din("lamv", [1, 256])
    subg = din("subg", [128, 1])
    t5t = din("t5t", [32, 8])
    c_ohc = din("c_ohc", [31, 4096])
    c_colmask = din("c_colmask", [1, 4096])
    c_rm = din("c_rm", [3, NQT, 2, 8 * 512], BF16)
    c_rmoh = din("c_rmoh", [2, 128], BF16)
    c_t5oh = din("c_t5oh", [32, NT5 * T5C])
    c_ident = din("c_ident", [128, 128], BF16)
    yout = nc.dram_tensor("yout", [3, T, D], F32, kind="ExternalOutput")

    tfull = dscr("tfull", [368, 4096], F32)
    zt5 = dscr("zt5", [NT5 * 8, 130, T5C], F32)
    qT0 = dscr("qT0", [3, 8, 128, TE2], BF16)
    kT0 = dscr("kT0", [3, 8, 128, TE2], BF16)
    v0 = dscr("v0", [3, TE2, D], BF16)
    aT = [dscr("aT0", [3, 8, 128, T], BF16), dscr("aT1", [3, 8, 128, T], BF16)]
    x1 = dscr("x1", [3, T, D], F32)
    x2 = dscr("x2", [3, T, D], F32)
    x3 = dscr("x3", [3, T, D], F32)
    actT = dscr("actT", [3, NFC, 128, T], BF16)
    qT1 = dscr("qT1", [3, 8, 128, T], BF16)
    kT1 = dscr("kT1", [2, 8, 128, T], BF16)
    v1 = dscr("v1", [2, T, D], BF16)
    kT1s = dscr("kT1s", [8 * 128, T], BF16)
    v1s = dscr("v1s", [T, D], BF16)
    kTg = dscr("kTg", [8 * 8 * 128, T], BF16)
    vg = dscr("vg", [8 * T, D], BF16)

    ps = [nc.alloc_psum_tensor(f"ps{i}", [128, 512], F32) for i in range(8)]
    psb = [p.bitcast(BF16) for p in ps]

    idt = A.t([128, 128], BF16)
    ones_b = A.t([128, 128], BF16)
    ones_f = A.t([128, 128], F32)
    epst = A.t([128, 1], F32)
    neglam = A.t([128, 1], F32)
    gsc = A.t([128, 1], F32)
    t5c = A.t([128, NT5 * 8, 2], F32)
    A.persist()

    S.add("sp", I("dma_start", out=idt[:, :], in_=c_ident[:, :]), writes=["idt"], dma=True)
    S.add("pool", I("memset", ones_b[:, :], 1.0), writes=["ones_b"])
    S.add("pool", I("memset", ones_f[:, :], 1.0), writes=["ones_f"])
    S.add("pool", I("memset", epst[:, :], 1e-6), writes=["eps"])

    cnt = [0]

    def uid():
        cnt[0] += 1
        return cnt[0]

    def phase0():
        A.reset()
        rp = A.t([31, 368], F32)
        ohc = A.t([31, 4096], F32)
        cm = A.t([1, 4096], F32)
        st = [A.t([128, 512], F32) for _ in range(2)]
        S.add("sp", I("dma_start", out=rp[:, :], in_=rpbT[:, :]), writes=["rp"], dma=True)
        S.add("sp", I("dma_start", out=ohc[:, :], in_=c_ohc[:, :]), writes=["ohc"], dma=True)
        S.add("sp", I("dma_start", out=cm[:, :], in_=c_colmask[:, :]), writes=["cm"], dma=True)
        i = 0
        for m0 in range(0, 368, 128):
            M = min(128, 368 - m0)
            for n in range(8):
                b = i % 2
                i += 1
                S.add("pe", I("matmul", ps[b][0:M, :], lhsT=rp[:, m0:m0 + M], rhs=ohc[:, n * 512:(n + 1) * 512], start=True, stop=False),
                    reads=["rp", "ohc"], writes=[("ps", b)])
                S.add("pe", I("matmul", ps[b][0:M, :], lhsT=ones_f[0:1, 0:M], rhs=cm[0:1, n * 512:(n + 1) * 512], start=False, stop=True),
                    reads=["ones_f", "cm"], writes=[("ps", b)])
                S.add("dve", I("tensor_copy", out=st[b][0:M, :], in_=ps[b][0:M, :]),
                      reads=[("ps", b)], writes=[("st", b)])
                S.add("sp", I("dma_start", out=tfull[m0:m0 + M, n * 512:(n + 1) * 512], in_=st[b][0:M, :]),
                    reads=[("st", b)], writes=["tfull"], dma=True)
        tt = A.t([32, 8], F32)
        tb = A.t([32, 8, 128], F32)
        oh5 = A.t([32, 4 * T5C], F32)
        S.add("sp", I("dma_start", out=tt[:, :], in_=t5t[:, :]), writes=["tt"], dma=True)
        for h in range(8):
            S.add("dve", I("tensor_copy", out=tb[:, h, :], in_=tt[:, h:h + 1].to_broadcast([32, 128])),
                  reads=["tt"], writes=["tb"])
        zst = [A.t([128, T5C], F32) for _ in range(2)]
        j = 0
        for tab in range(NT5):
            s4 = tab % 4
            S.add("sp", I("dma_start", out=oh5[:, s4 * T5C:(s4 + 1) * T5C], in_=c_t5oh[:, tab * T5C:(tab + 1) * T5C]),
                writes=[("oh5", s4)], dma=True)
            for h in range(8):
                zb = j % 2
                j += 1
                for n in range(3):
                    b = i % 2
                    i += 1
                    S.add("pe", I("matmul", ps[b][:, :], lhsT=tb[:, h, :], rhs=oh5[:, s4 * T5C + n * 512: s4 * T5C + (n + 1) * 512],
                        start=True, stop=True), reads=["tb", ("oh5", s4)], writes=[("ps", b)])
                    S.add("dve", I("tensor_copy", out=zst[zb][:, n * 512:(n + 1) * 512], in_=ps[b][:, :]),
                        reads=[("ps", b)], writes=[("zst", zb)])
                S.add("act", I("activation", out=t5c[:, tab * 8 + h, 0:1], in_=zst[zb][:, T5C - 1:T5C], func=AF.Copy),
                    reads=[("zst", zb)], writes=["t5c"])
                S.add("act", I("activation", out=t5c[:, tab * 8 + h, 1:2], in_=zst[zb][:, 1:2], func=AF.Copy),
                    reads=[("zst", zb)], writes=["t5c"])
                S.add("sp", I("dma_start", out=zt5[tab * 8 + h, 0:128, :], in_=zst[zb][:, :]),
                    reads=[("zst", zb)], writes=["zt5"], dma=True)
                S.add("sp", I("dma_start", out=zt5[tab * 8 + h, 128:130, :], in_=zst[zb][0:2, :]),
                    reads=[("zst", zb)], writes=["zt5"], dma=True)
        lv = A.t([1, 256], F32)
        lp = A.t([1, 128], F32)
        l2 = A.t([1, 2], F32)
        sg = A.t([128, 1], F32)
        lam_init = 0.8 - 0.6 * math.exp(-0.3 * 1)
        S.add("sp", I("dma_start", out=lv[:, :], in_=lamv[:, :]), writes=["lv"], dma=True)
        S.add("sp", I("dma_start", out=sg[:, :], in_=subg[:, :]), writes=["sg"], dma=True)
        S.add("dve", I("tensor_tensor", out=lp[:, 0:64], in0=lv[:, 0:64], in1=lv[:, 64:128], op=ALU.mult),
              reads=["lv"], writes=["lp"])
        S.add("dve", I("tensor_tensor", out=lp[:, 64:128], in0=lv[:, 128:192], in1=lv[:, 192:256], op=ALU.mult),
              reads=["lv", "lp"], writes=["lp"])
        S.add("dve", I("reduce_sum", out=l2[:, 0:1], in_=lp[:, 0:64], axis=mybir.AxisListType.X),
              reads=["lp"], writes=["l2"])
        S.add("dve", I("reduce_sum", out=l2[:, 1:2], in_=lp[:, 64:128], axis=mybir.AxisListType.X),
              reads=["lp", "l2"], writes=["l2"])
        S.add("act", I("activation", out=l2[:, :], in_=l2[:, :], func=AF.Exp), reads=["l2"], writes=["l2"])
        S.add("dve", I("scalar_tensor_tensor", out=lp[:, 0:1], in0=l2[:, 1:2], scalar=-lam_init, in1=l2[:, 0:1],
                                                      op0=ALU.add, op1=ALU.subtract),
              reads=["l2", "lp"], writes=["lp"])
        S.add("pe", I("matmul", ps[0][:, 0:1], lhsT=ones_f[0:1, :], rhs=lp[0:1, 0:1], start=True, stop=True),
              reads=["lp", "ones_f"], writes=[("ps", 0)])
        S.add("dve", I("tensor_copy", out=neglam[:, :], in_=ps[0][:, 0:1]), reads=[("ps", 0)], writes=["neglam"])
        S.add("dve", I("tensor_scalar", out=gsc[:, :], in0=sg[:, :], scalar1=1.0 - lam_init, scalar2=None,
                                               op0=ALU.mult), reads=["sg"], writes=["gsc"])

    def load_weight(wb, wap, nchunk, ncol, gcol=None, stg=None, qscale_cols=0, tag="w"):
        for c in range(nchunk):
            sb = c % len(stg)
            S.add("sp", I("dma_start", out=stg[sb][:, 0:ncol], in_=wap[c * 128:(c + 1) * 128, :]),
                  writes=[("wst", sb)], dma=True)
            eng = "dve" if c % 2 == 0 else "pool"
            if gcol is None:
                S.add(eng, I("tensor_copy", out=wb[:, c, :], in_=stg[sb][:, 0:ncol]),
                      reads=[("wst", sb)], writes=[(tag, c)])
            else:
                if qscale_cols:
                    S.add(eng, I("tensor_scalar", out=wb[:, c, 0:qscale_cols], in0=stg[sb][:, 0:qscale_cols], scalar1=gcol[:, c:c + 1],
                        scalar2=0.125, op0=ALU.mult, op1=ALU.mult), reads=[("wst", sb), "gcol"], writes=[(tag, c)])
                S.add(eng, I("tensor_scalar", out=wb[:, c, qscale_cols:ncol], in0=stg[sb][:, qscale_cols:ncol], scalar1=gcol[:, c:c + 1],
                    scalar2=None, op0=ALU.mult), reads=[("wst", sb), "gcol"], writes=[(tag, c)])

    def norm_transpose(xt_ap, xt_id, hb, hb_id, msb, ms_id, hT, hT_id, col0, junk, pst, extra_reads=()):
        S.add("act", I("activation", out=junk[:, :], in_=xt_ap, func=AF.Square, scale=1.0 / 32.0,
                                            accum_out=msb[:, 0:1]),
              reads=[xt_id] + list(extra_reads), writes=["junk", ms_id])
        S.add("act", I("activation", out=msb[:, 1:2], in_=msb[:, 0:1], func=AF.Ln, bias=epst[:, :]),
              reads=[ms_id, "eps"], writes=[ms_id])
        S.add("act", I("activation", out=msb[:, 2:3], in_=msb[:, 1:2], func=AF.Exp, scale=-0.5),
              reads=[ms_id], writes=[ms_id])
        S.add("act", I("activation", out=hb[:, :], in_=xt_ap, func=AF.Copy, scale=msb[:, 2:3]),
              reads=[xt_id, ms_id], writes=[hb_id])
        for c in range(8):
            S.add("pe", I("transpose", out=psb[pst][:, c * 128:(c + 1) * 128],
                                                   in_=hb[:, c * 128:(c + 1) * 128], identity=idt[:, :]),
                  reads=[hb_id, "idt"], writes=[("ps", pst)])
        S.add("dve", I("tensor_copy", out=hT[:, :, col0:col0 + 128], in_=psb[pst][:, :].rearrange("p (c n) -> p c n", c=8)),
            reads=[("ps", pst)], writes=[hT_id])

    def phase_qkv(layer):
        A.reset()
        wb = A.t([128, 8, 3 * D], BF16)
        stg = [A.t([128, 3 * D], F32) for _ in range(2)]
        gcol = A.t([128, 8], F32)
        S.add("sp", I("dma_start", out=gcol[:, :], in_=g_mix_pre[layer, :, :]), writes=["gcol"], dma=True)
        load_weight(wb, w_qkv[layer], 8, 3 * D, gcol=gcol, stg=stg, qscale_cols=D, tag="wqkv")
        xt = [A.t([128, D], F32) for _ in range(3)]
        hb = [A.t([128, D], BF16) for _ in range(2)]
        msb = [A.t([128, 4], F32) for _ in range(2)]
        junk = A.t([128, D], BF16)
        hT = [A.t([128, 8, 512], BF16) for _ in range(2)]
        oqk = [A.t([128, 16, 512], BF16) for _ in range(2)]
        ov = [A.t([128, 4, D], BF16) for _ in range(2)]
        wall = [("wqkv", c) for c in range(8)]
        src = xin if layer == 0 else x2
        it = 0
        evi = 0
        for seg in range(3):
            for g in range((TES[seg] if layer == 0 else T) // 512):
                gs = (seg * 100 + g) % 2
                for j in range(4):
                    it += 1
                    xs, hs = it % 3, it % 2
                    tok0 = g * 512 + j * 128
                    S.add("sp", I("dma_start", out=xt[xs][:, :], in_=src[seg, tok0:tok0 + 128, :]), writes=[("xt", xs)], dma=True)
                    norm_transpose(xt[xs][:, :], ("xt", xs), hb[hs], ("hb", hs), msb[hs], ("ms", hs), hT[gs],
                                   ("hT", gs, j), j * 128, junk, 6 + (it % 2))
                hTall = [("hT", gs, j) for j in range(4)]
                for m in range(16):
                    b = evi % 4
                    for c in range(8):
                        S.add("pe", I("matmul", ps[b][:, :], lhsT=wb[:, c, m * 128:(m + 1) * 128], rhs=hT[gs][:, c, :],
                            start=(c == 0), stop=(c == 7)), reads=[("wqkv", c)] + hTall, writes=[("ps", b)])
                    ee = "dve" if evi % 2 == 0 else "act"
                    evi += 1
                    if ee == "dve":
                        S.add("dve", I("tensor_copy", out=oqk[gs][:, m, :], in_=ps[b][:, :]),
                              reads=[("ps", b)], writes=[("oqk", gs, m)])
                    else:
                        S.add("act", I("activation", out=oqk[gs][:, m, :], in_=ps[b][:, :],
                                                                             func=AF.Copy),
                              reads=[("ps", b)], writes=[("oqk", gs, m)])
                qd = qT0 if layer == 0 else qT1
                S.add("pool", I("dma_start", out=qd[seg, :, :, g * 512:(g + 1) * 512].rearrange("c p n -> p c n"), in_=oqk[gs][:, 0:8, :]),
                    reads=[("oqk", gs, m) for m in range(8)], writes=[("qT", layer, seg)], dma=True)
                if layer == 0:
                    kdst = kT0[seg, :, :, g * 512:(g + 1) * 512].rearrange("c p n -> p c n")
                elif seg < 2:
                    kdst = kT1[seg, :, :, g * 512:(g + 1) * 512].rearrange("c p n -> p c n")
                else:
                    kdst = kT1s[:, g * 512:(g + 1) * 512].rearrange("(c p) n -> p c n", p=128)
                S.add("pool", I("dma_start", out=kdst, in_=oqk[gs][:, 8:16, :]),
                      reads=[("oqk", gs, m) for m in range(8, 16)], writes=[("kT", layer, seg)], dma=True)
                for j in range(4):
                    for hf in range(2):
                        b = evi % 4
                        for c in range(8):
                            S.add("pe", I("matmul", ps[b][:, :], lhsT=hT[gs][:, c, j * 128:(j + 1) * 128],
                                rhs=wb[:, c, 2 * D + hf * 512: 2 * D + (hf + 1) * 512],
                                start=(c == 0), stop=(c == 7)), reads=[("wqkv", c), ("hT", gs, j)], writes=[("ps", b)])
                        ee = "dve" if evi % 2 == 0 else "act"
                        evi += 1
                        if ee == "dve":
                            S.add("dve", I("tensor_copy", out=ov[gs][:, j, hf * 512:(hf + 1) * 512], in_=ps[b][:, :]),
                                reads=[("ps", b)], writes=[("ov", gs, j, hf)])
                        else:
                            S.add("act", I("activation", out=ov[gs][:, j, hf * 512:(hf + 1) * 512], in_=ps[b][:, :], func=AF.Copy),
                                reads=[("ps", b)], writes=[("ov", gs, j, hf)])
                if layer == 0:
                    vdst = v0[seg, g * 512:(g + 1) * 512, :]
                elif seg < 2:
                    vdst = v1[seg, g * 512:(g + 1) * 512, :]
                else:
                    vdst = v1s[g * 512:(g + 1) * 512, :]
                S.add("pool", I("dma_start", out=vdst.rearrange("(j p) n -> p j n", p=128), in_=ov[gs][:, :, :]),
                    reads=[("ov", gs, j, hf) for j in range(4) for hf in range(2)], writes=[("v", layer, seg)],
                    dma=True)

    def phase_na():
        A.reset()
        rmoh = A.t([2, 128], BF16)
        S.add("sp", I("dma_start", out=rmoh[:, :], in_=c_rmoh[:, :]), writes=["rmoh"], dma=True)
        bc = [A.t([128, 16, 512], F32) for _ in range(1)]
        QT = [A.t([128, T], BF16) for _ in range(2)]
        KT = [A.t([128, TE2], BF16) for _ in range(2)]
        VT = [A.t([128, NKT, 128], BF16) for _ in range(2)]
        rmt = [A.t([2, 8 * 512], BF16) for _ in range(2)]
        PT = [A.t([128, 512], BF16) for _ in range(3)]
        rs = [A.t([128, 512], F32) for _ in range(2)]
        ao = [A.t([128, 512], BF16) for _ in range(2)]
        li = 0
        qi = 0
        ui = 0
        for hp in range(8):
            bs = 0
            for hh in range(2):
                h = 2 * hp + hh
                for t in range(8):
                    for a in range(2):
                        drr = 15 - 2 * t - a
                        srcap = bass.AP(tfull, (h * 23 + drr) * 4096, [[64, 64], [4096, 8], [1, 64]])
                        S.add("sp", I("dma_start", out=bc[bs][a * 64:(a + 1) * 64, hh * 8 + t, :].rearrange("p (b c) -> p b c", b=8),
                            in_=srcap), reads=["tfull"], writes=[("bc", bs, hh, t)], dma=True)
            for seg in range(3):
                li += 1
                ls = li % 2
                if seg < 2:
                    S.add("sp", I("dma_start", out=QT[ls][:, :], in_=qT0[seg, hp, :, 256:256 + T]),
                        reads=[("qT", 0, seg)], writes=[("QT", ls)], dma=True)
                else:
                    S.add("sp", I("dma_start", out=QT[ls][:, 0:CH], in_=qT0[seg, hp, :, 256:256 + CH]),
                        reads=[("qT", 0, seg)], writes=[("QT", ls)], dma=True)
                    S.add("sp", I("dma_start", out=QT[ls][:, CH:T], in_=qT0[seg, hp, :, 768 + CH:768 + T]),
                        reads=[("qT", 0, seg), ("QT", ls)], writes=[("QT", ls)], dma=True)
                tes = TES[seg]
                S.add("sp", I("dma_start", out=KT[ls][:, 0:tes], in_=kT0[seg, hp, :, 0:tes]),
                    reads=[("kT", 0, seg)], writes=[("KT", ls)], dma=True)
                S.add("sp", I("dma_start", out=VT[ls][:, 0:tes // 128, :],
                    in_=v0[seg, 0:tes, hp * 128:(hp + 1) * 128].rearrange("(t p) c -> p t c", p=128)),
                    reads=[("v", 0, seg)], writes=[("VT", ls)], dma=True)
                for qt in range(NQT):
                    qi += 1
                    rsl = qi % 2
                    kb0 = 512 * qt + (512 if (seg == 2 and qt >= NQH) else 0)
                    S.add("sp", I("dma_start", out=rmt[rsl][:, :], in_=c_rm[seg, qt, :, :]), writes=[("rmt", rsl)], dma=True)
                    for hh in range(2):
                        po, pS = 4 + hh, 6 + hh
                        pr = slice(64 * hh, 64 * hh + 64)

                        def qk(t, sb):
                            S.add("pe", I("matmul", ps[sb][:, :], lhsT=KT[ls][pr, kb0 + t * 128: kb0 + (t + 1) * 128],
                                rhs=QT[ls][pr, qt * 512:(qt + 1) * 512], start=True, stop=False),
                                reads=[("KT", ls), ("QT", ls)], writes=[("ps", sb)])
                            S.add("pe", I("matmul", ps[sb][:, :], lhsT=rmoh[:, :], rhs=rmt[rsl][:, t * 512:(t + 1) * 512],
                                start=False, stop=True), reads=["rmoh", ("rmt", rsl)], writes=[("ps", sb)])

                        sbs = []
                        for t in range(8):
                            sbs.append((ui + t) % 3)
                        qk(0, sbs[0])
                        for t in range(8):
                            sb = sbs[t]
                            ptb = (ui + t) % 3
                            if t + 1 < 8:
                                qk(t + 1, sbs[t + 1])
                            S.add("dve", I("tensor_tensor", out=ps[sb][:, :], in0=ps[sb][:, :], in1=bc[bs][:, hh * 8 + t, :], op=ALU.add),
                                reads=[("ps", sb), ("bc", bs, hh, t)], writes=[("ps", sb)])
                            S.add("act", I("activation", out=PT[ptb][:, :], in_=ps[sb][:, :], func=AF.Exp),
                                reads=[("ps", sb)], writes=[("PT", ptb)])
                            S.add("pe", I("matmul", ps[po][:, :], lhsT=VT[ls][:, kb0 // 128 + t, :], rhs=PT[ptb][:, :],
                                start=(t == 0), stop=(t == 7)), reads=[("VT", ls), ("PT", ptb)], writes=[("ps", po)])
                            S.add("pe", I("matmul", ps[pS][:, :], lhsT=ones_b[:, :], rhs=PT[ptb][:, :],
                                start=(t == 0), stop=(t == 7)), reads=["ones_b", ("PT", ptb)], writes=[("ps", pS)])
                        ui += 8
                        S.add("dve", I("reciprocal", out=rs[hh][pr, :], in_=ps[pS][pr, :]),
                              reads=[("ps", pS)], writes=[("rs", hh)])
                        S.add("dve", I("tensor_tensor", out=ao[rsl][pr, :], in0=ps[po][pr, :], in1=rs[hh][pr, :], op=ALU.mult),
                            reads=[("ps", po), ("rs", hh)], writes=[("ao", rsl, hh)])
                    S.add("pool", I("dma_start", out=aT[0][seg, hp, :, qt * 512:(qt + 1) * 512], in_=ao[rsl][:, :]),
                        reads=[("ao", rsl, 0), ("ao", rsl, 1)], writes=[("aT", 0, seg)], dma=True)

    def phase_a(layer, xsrc, xsrc_off, xdst):
        A.reset()
        wo = A.t([128, 8, D], BF16)
        wg = A.t([128, 8, DFF], BF16)
        wu = A.t([128, 8, DFF], BF16)
        stg = [A.t([128, DFF], F32) for _ in range(2)]
        gcol = A.t([128, 8], F32)
        gpost = A.t([128, D], F32)
        S.add("sp", I("dma_start", out=gcol[:, :], in_=g_ffn_pre[layer, :, :]), writes=["gcol"], dma=True)
        S.add("sp", I("dma_start", out=gpost[:, :], in_=g_mix_post[layer:layer + 1, :].to_broadcast([128, D])),
              writes=["gpost"], dma=True)
        load_weight(wo, w_o[layer], 8, D, stg=stg, tag="wo")
        load_weight(wg, w_gate[layer], 8, DFF, gcol=gcol, stg=stg, tag="wg")
        load_weight(wu, w_up[layer], 8, DFF, gcol=gcol, stg=stg, tag="wu")
        at = [A.t([128, 8, 512], BF16) for _ in range(2)]
        xt = [A.t([128, D], F32) for _ in range(2)]
        xm = [A.t([128, D], F32) for _ in range(2)]
        hb = [A.t([128, D], BF16) for _ in range(2)]
        msb = [A.t([128, 8], F32) for _ in range(2)]
        junk = A.t([128, D], BF16)
        hT = [A.t([128, 8, 512], BF16) for _ in range(2)]
        sg = [A.t([128, 512], F32) for _ in range(2)]
        oa = [A.t([128, 2, 512], BF16) for _ in range(2)]
        groups = [(seg, g) for seg in range(3) for g in range(T // 512)]
        ci = [0]

        def stage_a(k, j):
            seg, g = groups[k]
            gs = k % 2
            s2 = (4 * k + j) % 2
            tok0 = g * 512 + j * 128
            if j == 0:
                S.add("sp", I("dma_start", out=at[gs][:, :, :], in_=aT[layer][seg, :, :, g * 512:(g + 1) * 512].rearrange("c p n -> p c n")),
                    reads=[("aT", layer, seg)], writes=[("at", gs)], dma=True)
            xo_ = xsrc_off + tok0 + (512 if (xsrc_off and seg == 2 and tok0 >= CH) else 0)
            S.add("sp", I("dma_start", out=xt[s2][:, :], in_=xsrc[seg, xo_: xo_ + 128, :]),
                reads=[("xsrc", seg)], writes=[("xt", s2)], dma=True)
            banks = ((0, 0), (1, 1))
            for hf, pbk in banks:
                for c in range(8):
                    S.add("pe", I("matmul", ps[pbk][:, :], lhsT=at[gs][:, c, j * 128:(j + 1) * 128],
                        rhs=wo[:, c, hf * 512:(hf + 1) * 512], start=(c == 0), stop=(c == 7)),
                        reads=[("at", gs), ("wo", c)], writes=[("ps", pbk)])
            m = msb[s2]
            mid = ("ms", s2)
            for hf, pbk in banks:
                S.add("act", I("activation", out=junk[:, 0:512], in_=ps[pbk][:, :], func=AF.Square, scale=1.0 / 32.0,
                    accum_out=m[:, hf:hf + 1]), reads=[("ps", pbk)], writes=["junk", mid])
            S.add("dve", I("tensor_tensor", out=m[:, 2:3], in0=m[:, 0:1], in1=m[:, 1:2], op=ALU.add),
                  reads=[mid], writes=[mid])
            S.add("act", I("activation", out=m[:, 3:4], in_=m[:, 2:3], func=AF.Ln, bias=epst[:, :]),
                  reads=[mid, "eps"], writes=[mid])
            S.add("act", I("activation", out=m[:, 4:5], in_=m[:, 3:4], func=AF.Exp, scale=-0.5),
                  reads=[mid], writes=[mid])
            for hf, pbk in banks:
                S.add("dve", I("scalar_tensor_tensor", out=xm[s2][:, hf * 512:(hf + 1) * 512], in0=ps[pbk][:, :], scalar=m[:, 4:5],
                    in1=gpost[:, hf * 512:(hf + 1) * 512], op0=ALU.mult, op1=ALU.mult),
                    reads=[("ps", pbk), mid, "gpost"], writes=[("xm", s2, hf)])
            S.add("pool", I("tensor_tensor", out=xm[s2][:, :], in0=xm[s2][:, :], in1=xt[s2][:, :], op=ALU.add),
                  reads=[("xm", s2, 0), ("xm", s2, 1), ("xt", s2)], writes=[("xm", s2, 0), ("xm", s2, 1)])
            S.add("pool", I("dma_start", out=xdst[seg, tok0:tok0 + 128, :], in_=xm[s2][:, :]),
                reads=[("xm", s2, 0), ("xm", s2, 1)], writes=[("xdst", seg)], dma=True)

        def stage_b(k, j):
            s2 = (4 * k + j) % 2
            m2 = msb[s2]
            mid = ("ms", s2)
            S.add("act", I("activation", out=junk[:, :], in_=xm[s2][:, :], func=AF.Square, scale=1.0 / 32.0, accum_out=m2[:, 5:6]),
                reads=[("xm", s2, 0), ("xm", s2, 1)], writes=["junk", mid])
            S.add("act", I("activation", out=m2[:, 6:7], in_=m2[:, 5:6], func=AF.Ln, bias=epst[:, :]),
                  reads=[mid, "eps"], writes=[mid])
            S.add("act", I("activation", out=m2[:, 7:8], in_=m2[:, 6:7], func=AF.Exp, scale=-0.5),
                  reads=[mid], writes=[mid])
            S.add("act", I("activation", out=hb[s2][:, :], in_=xm[s2][:, :], func=AF.Copy, scale=m2[:, 7:8]),
                  reads=[("xm", s2, 0), ("xm", s2, 1), mid], writes=[("hb", s2)])

        def stage_c(k, j):
            gs = k % 2
            s2 = (4 * k + j) % 2
            pst = 2 + s2
            for c in range(8):
                S.add("pe", I("transpose", out=psb[pst][:, c * 128:(c + 1) * 128], in_=hb[s2][:, c * 128:(c + 1) * 128],
                    identity=idt[:, :]), reads=[("hb", s2), "idt"], writes=[("ps", pst)])
            S.add("dve", I("tensor_copy", out=hT[gs][:, :, j * 128:(j + 1) * 128],
                in_=psb[pst][:, :].rearrange("p (c n) -> p c n", c=8)),
                reads=[("ps", pst)], writes=[("hT", gs, j)])

        def ffn_chunk(k, fc):
            seg, g = groups[k]
            gs = k % 2
            hTall = [("hT", gs, j) for j in range(4)]
            ci[0] += 1
            pg, pu = (6, 7) if fc % 2 == 0 else (4, 5)
            for c in range(8):
                S.add("pe", I("matmul", ps[pg][:, :], lhsT=wg[:, c, fc * 128:(fc + 1) * 128], rhs=hT[gs][:, c, :],
                    start=(c == 0), stop=(c == 7)), reads=[("wg", c)] + hTall, writes=[("ps", pg)])
            for c in range(8):
                S.add("pe", I("matmul", ps[pu][:, :], lhsT=wu[:, c, fc * 128:(fc + 1) * 128], rhs=hT[gs][:, c, :],
                    start=(c == 0), stop=(c == 7)), reads=[("wu", c)] + hTall, writes=[("ps", pu)])
            sgb = ci[0] % 2
            os_ = ((ci[0] - 1) // 2) % 2
            S.add("act", I("activation", out=sg[sgb][:, :], in_=ps[pg][:, :], func=AF.Exp, scale=-1.0),
                  reads=[("ps", pg)], writes=[("sg", sgb)])
            S.add("act", I("activation", out=sg[sgb][:, :], in_=sg[sgb][:, :], func=AF.Ln, bias=ones_f[:, 0:1]),
                  reads=[("sg", sgb), "ones_f"], writes=[("sg", sgb)])
            S.add("act", I("activation", out=sg[sgb][:, :], in_=sg[sgb][:, :], func=AF.Exp, scale=-1.0),
                  reads=[("sg", sgb)], writes=[("sg", sgb)])
            S.add("dve", I("tensor_tensor", out=sg[sgb][:, :], in0=ps[pg][:, :], in1=sg[sgb][:, :], op=ALU.mult),
                  reads=[("ps", pg), ("sg", sgb)], writes=[("sg", sgb)])
            S.add("dve", I("tensor_tensor", out=oa[os_][:, fc % 2, :], in0=ps[pu][:, :], in1=sg[sgb][:, :], op=ALU.mult),
                reads=[("ps", pu), ("sg", sgb)], writes=[("oa", os_, fc % 2)])
            if fc % 2 == 1:
                S.add("pool", I("dma_start", out=actT[seg, fc - 1:fc + 1, :, g * 512:(g + 1) * 512].rearrange("c p n -> p c n"),
                    in_=oa[os_][:, :, :]), reads=[("oa", os_, 0), ("oa", os_, 1)],
                    writes=[("actT", seg)], dma=True)

        for j in range(4):
            stage_a(0, j)
            stage_b(0, j)
            stage_c(0, j)
        for k in range(len(groups)):
            nxt = k + 1 < len(groups)
            for fc in range(NFC):
                ffn_chunk(k, fc)
                if nxt and fc < 20:
                    j, r = divmod(fc, 5)
                    if r == 0:
                        stage_a(k + 1, j)
                    elif r == 2:
                        stage_b(k + 1, j)
                    elif r == 4:
                        stage_c(k + 1, j)

    def phase_b(layer, xmid, xdst):
        A.reset()
        wd = A.t([128, NFC, D], BF16)
        stg = [A.t([128, D], F32) for _ in range(3)]
        gpost = A.t([128, D], F32)
        S.add("sp", I("dma_start", out=gpost[:, :], in_=g_ffn_post[layer:layer + 1, :].to_broadcast([128, D])),
              writes=["gpost"], dma=True)
        load_weight(wd, w_down[layer], NFC, D, stg=stg, tag="wd")
        wdall = [("wd", c) for c in range(NFC)]
        at = [A.t([128, NFC, 512], BF16) for _ in range(2)]
        xt = [A.t([128, D], F32) for _ in range(3)]
        xo = [A.t([128, D], F32) for _ in range(2)]
        msb = [A.t([128, 8], F32) for _ in range(2)]
        junk = A.t([128, 512], BF16)
        it = 0
        gi = 0
        for seg in range(3):
            for g in range(T // 512):
                gi += 1
                gs = gi % 2
                for hf in range(2):
                    S.add("sp", I("dma_start", out=at[gs][:, hf * 11:(hf + 1) * 11, :],
                        in_=actT[seg, hf * 11:(hf + 1) * 11, :, g * 512:(g + 1) * 512].rearrange("c p n -> p c n")),
                        reads=[("actT", seg)], writes=[("at", gs, hf)], dma=True)
                for j in range(4):
                    it += 1
                    s2, s3 = it % 2, it % 3
                    tok0 = g * 512 + j * 128
                    S.add("sp", I("dma_start", out=xt[s3][:, :], in_=xmid[seg, tok0:tok0 + 128, :]),
                        reads=[("xdst", seg)], writes=[("xt", s3)], dma=True)
                    pa, pb_ = (0, 1) if s2 == 0 else (2, 3)
                    for hf, pbk in ((0, pa), (1, pb_)):
                        for c in range(NFC):
                            S.add("pe", I("matmul", ps[pbk][:, :], lhsT=at[gs][:, c, j * 128:(j + 1) * 128],
                                rhs=wd[:, c, hf * 512:(hf + 1) * 512], start=(c == 0), stop=(c == NFC - 1)),
                                reads=[("at", gs, 0), ("at", gs, 1), ("wd", c)], writes=[("ps", pbk)])
                    m = msb[s2]
                    mid = ("ms", s2)
                    for hf, pbk in ((0, pa), (1, pb_)):
                        S.add("act", I("activation", out=junk[:, 0:512], in_=ps[pbk][:, :], func=AF.Square, scale=1.0 / 32.0,
                            accum_out=m[:, hf:hf + 1]), reads=[("ps", pbk)], writes=["junk", mid])
                    S.add("dve", I("tensor_tensor", out=m[:, 2:3], in0=m[:, 0:1], in1=m[:, 1:2], op=ALU.add),
                          reads=[mid], writes=[mid])
                    S.add("act", I("activation", out=m[:, 3:4], in_=m[:, 2:3], func=AF.Ln, bias=epst[:, :]),
                          reads=[mid, "eps"], writes=[mid])
                    S.add("act", I("activation", out=m[:, 4:5], in_=m[:, 3:4], func=AF.Exp, scale=-0.5),
                          reads=[mid], writes=[mid])
                    for hf, pbk in ((0, pa), (1, pb_)):
                        S.add("dve", I("scalar_tensor_tensor", out=xo[s2][:, hf * 512:(hf + 1) * 512], in0=ps[pbk][:, :], scalar=m[:, 4:5],
                            in1=gpost[:, hf * 512:(hf + 1) * 512], op0=ALU.mult, op1=ALU.mult),
                            reads=[("ps", pbk), mid, "gpost"], writes=[("xo", s2, hf)])
                    S.add("pool", I("tensor_tensor", out=xo[s2][:, :], in0=xo[s2][:, :], in1=xt[s3][:, :], op=ALU.add),
                        reads=[("xo", s2, 0), ("xo", s2, 1), ("xt", s3)], writes=[("xo", s2, 0), ("xo", s2, 1)])
                    S.add("pool", I("dma_start", out=xdst[seg, tok0:tok0 + 128, :], in_=xo[s2][:, :]),
                        reads=[("xo", s2, 0), ("xo", s2, 1)], writes=[("xnext", seg)], dma=True)

    def phase_diff():
        A.reset()
        NKB = 12
        NKT_C = CH // 128
        QT = [A.t([128, T], BF16) for _ in range(2)]
        KB = [A.t([128, CH], BF16) for _ in range(NKB)]
        VB = [A.t([128, NKT_C, 128], BF16) for _ in range(NKB)]
        bt = [A.t([128, 8, 512], F32) for _ in range(2)]
        PT = [A.t([128, 512], BF16) for _ in range(4)]
        tmp = [A.t([128, 512], F32) for _ in range(6)]
        ao = [A.t([128, 512], BF16) for _ in range(2)]
        kbi = 0
        hi = 0
        ui = 0
        qi = 0
        for seg in range(3):
            for h in range(8):
                hi += 1
                hs = hi % 2
                S.add("sp", I("dma_start", out=QT[hs][:, :], in_=qT1[seg, h, :, :]),
                      reads=[("qT", 1, seg)], writes=[("QT", hs)], dma=True)

                def load_bt(tab, slot, dlt, bsl):
                    off = (tab * 8 + h) * 130 * T5C + (T5C0 - dlt)
                    srcap = bass.AP(zt5, off, [[T5C - 1, 128], [1, 512]])
                    S.add("sp", I("dma_start", out=bt[bsl][:, slot, :], in_=srcap), reads=["zt5"], writes=[("bt", bsl, slot)], dma=True)

                for part in range(1 if seg < 2 else 2):
                    nch = 2 if seg < 2 else 8
                    chunks = []
                    for r in range(nch):
                        kb = kbi % NKB
                        kbi += 1
                        if seg < 2:
                            ksrc = kT1[seg, h, :, r * CH:(r + 1) * CH]
                            vsrc = v1[seg, r * CH:(r + 1) * CH, h * 128:(h + 1) * 128]
                            kdep, vdep = ("kT", 1, seg), ("v", 1, seg)
                        else:
                            ksrc = kTg[r * 1024 + h * 128: r * 1024 + (h + 1) * 128, part * CH:(part + 1) * CH]
                            vsrc = vg[r * T + part * CH: r * T + (part + 1) * CH, h * 128:(h + 1) * 128]
                            kdep, vdep = "kTg", "vg"
                        S.add("sp", I("dma_start", out=KB[kb][:, :], in_=ksrc),
                              reads=[kdep], writes=[("KB", kb)], dma=True)
                        S.add("sp", I("dma_start", out=VB[kb][:, :, :], in_=vsrc.rearrange("(t p) c -> p t c", p=128)),
                            reads=[vdep], writes=[("VB", kb)], dma=True)
                        chunks.append(kb)
                    qts = list(range(NQT)) if seg < 2 else list(range(part * NQH, (part + 1) * NQH))
                    for qt in qts:
                        qi += 1
                        O1, O2, S1, S2 = 4, 5, 6, 7
                        units = []
                        for r in range(nch):
                            for ktl in range(NKT_C):
                                if seg < 2:
                                    d = r * NKT_C + ktl - 4 * qt
                                    if -1 <= d <= 4:
                                        units.append((r, ktl, "tile", (0, d + 1, d * 128)))
                                    else:
                                        units.append((r, ktl, "const", (0, 0 if d < -1 else 1)))
                                else:
                                    qtl = qt - part * NQH
                                    d = ktl - 4 * qtl
                                    if -1 <= d <= 4:
                                        units.append((r, ktl, "tile", (1 + r, d + 1, d * 128)))
                                    elif qtl == 0 and ktl == NKT_C - 1:
                                        units.append((r, ktl, "tile", (9 + r, 6, -128)))
                                    elif qtl == NQH - 1 and ktl == 0:
                                        units.append((r, ktl, "tile", (17 + r, 7, 512)))
                                    else:
                                        units.append((r, ktl, "const", (1 + r, 0 if d < -1 else 1)))
                        nU = len(units)

                        def qk(u, sbp):
                            r, ktl, mode, arg = units[u]
                            kb = chunks[r]
                            for mp in range(2):
                                prr = slice(64 * mp, 64 * mp + 64)
                                S.add("pe", I("matmul", ps[sbp[mp]][:, :], lhsT=KB[kb][prr, ktl * 128:(ktl + 1) * 128],
                                    rhs=QT[hs][prr, qt * 512:(qt + 1) * 512], start=True, stop=True),
                                    reads=[("KB", kb), ("QT", hs)], writes=[("ps", sbp[mp])])

                        def sbpair(u):
                            return (0, 1) if (ui + u) % 2 == 0 else (2, 3)

                        qk(0, sbpair(0))
                        for u in range(nU):
                            r, ktl, mode, arg = units[u]
                            kb = chunks[r]
                            sbp = sbpair(u)
                            if u + 1 < nU:
                                qk(u + 1, sbpair(u + 1))
                            if mode == "tile":
                                tab, slot, dlt = arg
                                bsl = (ui + u) % 2
                                load_bt(tab, slot, dlt, bsl)
                                for mp in range(2):
                                    S.add("dve", I("tensor_tensor", out=ps[sbp[mp]][:, :], in0=ps[sbp[mp]][:, :], in1=bt[bsl][:, slot, :], op=ALU.add),
                                        reads=[("ps", sbp[mp]), ("bt", bsl, slot)], writes=[("ps", sbp[mp])])
                                biasap = None
                            else:
                                tab, side = arg
                                biasap = t5c[:, tab * 8 + h, side:side + 1]
                            for mp in range(2):
                                ptb = 2 * ((ui + u) % 2) + mp
                                if biasap is None:
                                    S.add("act", I("activation", out=PT[ptb][:, :], in_=ps[sbp[mp]][:, :], func=AF.Exp),
                                        reads=[("ps", sbp[mp])], writes=[("PT", ptb)])
                                else:
                                    S.add("act", I("activation", out=PT[ptb][:, :], in_=ps[sbp[mp]][:, :], func=AF.Exp, bias=biasap),
                                        reads=[("ps", sbp[mp]), "t5c"], writes=[("PT", ptb)])
                                S.add("pe", I("matmul", ps[O1 + mp][:, :], lhsT=VB[kb][:, ktl, :], rhs=PT[ptb][:, :],
                                    start=(u == 0), stop=(u == nU - 1)),
                                    reads=[("VB", kb), ("PT", ptb)], writes=[("ps", O1 + mp)])
                                S.add("pe", I("matmul", ps[S1 + mp][:, :], lhsT=ones_b[:, :], rhs=PT[ptb][:, :],
                                    start=(u == 0), stop=(u == nU - 1)),
                                    reads=["ones_b", ("PT", ptb)], writes=[("ps", S1 + mp)])
                        ui += nU
                        r1, r2, aa, bb, uu, sq = tmp
                        S.add("dve", I("reciprocal", out=r1[:, :], in_=ps[S1][:, :]), reads=[("ps", S1)],
                              writes=["r1"])
                        S.add("dve", I("reciprocal", out=r2[:, :], in_=ps[S2][:, :]), reads=[("ps", S2)],
                              writes=["r2"])
                        S.add("dve", I("tensor_tensor", out=aa[:, :], in0=ps[O1][:, :], in1=r1[:, :], op=ALU.mult),
                              reads=[("ps", O1), "r1"], writes=["aa"])
                        S.add("dve", I("tensor_tensor", out=bb[:, :], in0=ps[O2][:, :], in1=r2[:, :], op=ALU.mult),
                              reads=[("ps", O2), "r2"], writes=["bb"])
                        S.add("dve", I("scalar_tensor_tensor", out=uu[:, :], in0=bb[:, :], scalar=neglam[:, 0:1], in1=aa[:, :], op0=ALU.mult, op1=ALU.add),
                            reads=["aa", "bb", "neglam"], writes=["uu"])
                        S.add("pool", I("tensor_tensor", out=sq[:, :], in0=uu[:, :], in1=uu[:, :], op=ALU.mult),
                              reads=["uu"], writes=["sq"])
                        S.add("pe", I("matmul", ps[S1][:, :], lhsT=ones_f[:, :], rhs=sq[:, :], start=True, stop=True),
                              reads=["ones_f", "sq"], writes=[("ps", S1)])
                        S.add("act", I("activation", out=r1[:, :], in_=ps[S1][:, :], func=AF.Ln, scale=1.0 / 128.0,
                                                            bias=epst[:, :]), reads=[("ps", S1), "eps"], writes=["r1"])
                        S.add("act", I("activation", out=r2[:, :], in_=r1[:, :], func=AF.Exp, scale=-0.5),
                              reads=["r1"], writes=["r2"])
                        aos = qi % 2
                        S.add("dve", I("scalar_tensor_tensor", out=ao[aos][:, :], in0=uu[:, :], scalar=gsc[:, 0:1], in1=r2[:, :], op0=ALU.mult, op1=ALU.mult),
                            reads=["uu", "r2", "gsc"], writes=[("ao", aos)])
                        S.add("pool", I("dma_start", out=aT[1][seg, h, :, qt * 512:(qt + 1) * 512], in_=ao[aos][:, :]),
                            reads=[("ao", aos)], writes=[("aT", 1, seg)], dma=True)

    def phase_gather():
        rg = [list(range(8))]
        S.add("pool", I("collective_compute", "AllGather", ALU.bypass, replica_groups=rg, ins=[kT1s[:, :]], outs=[kTg[:, :]]),
            reads=[("kT", 1, 2)], writes=["kTg"], dma=True, inc=1)
        S.add("pool", I("collective_compute", "AllGather", ALU.bypass, replica_groups=rg, ins=[v1s[:, :]], outs=[vg[:, :]]),
            reads=[("v", 1, 2)], writes=["vg"], dma=True, inc=1)

    phases = [
        ("p0", phase0),
        ("p1", lambda: phase_qkv(0)),
        ("p2", phase_na),
        ("p3a", lambda: phase_a(0, xin, 256, x1)),
        ("p3b", lambda: phase_b(0, x1, x2)),
        ("p4", lambda: phase_qkv(1)),
        ("pg", phase_gather if use_cc else (lambda: None)),
        ("p5", phase_diff),
        ("p6a", lambda: phase_a(1, x2, 0, x3)),
        ("p6b", lambda: phase_b(1, x3, yout)),
    ]
    for name, fn in phases:
        fn()
        S.barrier()
        if stop_after == name:
            break
    S.emit()
    return nc


def make_in_maps(inp, rows):
    T = rows * 64
    TE2 = T + 1024
    CH = T // 2
    xp = np.asarray(inp["x_prompt"], np.float32)
    xs = np.asarray(inp["x_sample"], np.float32)
    common = {
        "na_w_qkv": np.ascontiguousarray(inp["na_w_qkv"][0]), "diff_w_qkv": np.ascontiguousarray(inp["diff_w_qkv"][0]),
        "na_w_o": np.ascontiguousarray(inp["na_w_o"][0]), "diff_w_o": np.ascontiguousarray(inp["diff_w_o"][0]),
        "ffn_w_gate": np.ascontiguousarray(inp["ffn_w_gate"]), "ffn_w_up": np.ascontiguousarray(inp["ffn_w_up"]),
        "ffn_w_down": np.ascontiguousarray(inp["ffn_w_down"]),
        "g_mix_pre": np.stack([_gcol(np.asarray(inp["mix_pre_g"][i])) for i in range(2)]),
        "g_ffn_pre": np.stack([_gcol(np.asarray(inp["ffn_pre_g"][i])) for i in range(2)]),
        "g_mix_post": np.ascontiguousarray(inp["mix_post_g"]), "g_ffn_post": np.ascontiguousarray(inp["ffn_post_g"]),
        "subg": np.ascontiguousarray(np.asarray(inp["diff_subln_g"][0]).reshape(128, 1)),
        "t5t": np.ascontiguousarray(inp["t5_bias"]),
        "lamv": np.concatenate([np.asarray(inp[k][0]) for k in
                                ("diff_lambda_q1", "diff_lambda_k1", "diff_lambda_q2", "diff_lambda_k2")]).reshape(1, 256),
    }
    rpb = np.asarray(inp["na_rpb"][0], np.float32)
    rp = np.zeros((31, 16, 23), np.float32)
    for drr in range(4, 19):
        rp[:, :, drr] = rpb[:, 18 - drr, :].T
    common["rpbT"] = rp.reshape(31, 368)
    common = {k: np.ascontiguousarray(v, dtype=np.float32) for k, v in common.items()}
    maps = []
    for core in range(NCORES):
        m = dict(common)
        m.update(_static_tables(rows, core))
        x = np.zeros((3, TE2, D), np.float32)
        x[0, 256:256 + T] = xp[2 * core]
        x[1, 256:256 + T] = xp[2 * core + 1]
        for sq in range(2):
            lo, hi = core * CH - 256, core * CH + CH + 256
            a, b = max(lo, 0), min(hi, 4 * T)
            base = sq * (CH + 512)
            x[2, base + a - lo: base + a - lo + (b - a)] = xs[sq, a:b]
        m["xin"] = x
        maps.append(m)
    return maps


def run(inp, rows, debug_outs=(), use_cc=True, stop_after=None, trace=False):
    nc = build_program(rows, debug_outs=debug_outs, use_cc=use_cc, stop_after=stop_after)
    maps = make_in_maps(inp, rows)
    res = run_bass_kernel_spmd(nc, maps, core_ids=list(range(NCORES)), **({"trace": True} if trace else {}))
    return res


def kernel(**inputs):
    rows = 64
    T = rows * 64
    res = run(inputs, rows)
    yp = np.zeros((16, T, D), np.float32)
    ys = np.zeros((2, 4 * T, D), np.float32)
    for core in range(NCORES):
        y = res.results[core]["yout"]
        yp[2 * core] = y[0]
        yp[2 * core + 1] = y[1]
        ys[0, core * (T // 2):(core + 1) * (T // 2)] = y[2][:T // 2]
        ys[1, core * (T // 2):(core + 1) * (T // 2)] = y[2][T // 2:]
    return yp, ys
```
